# Optimizing a Trainium2 kernel written in Bass

```python
import math
import jax, jax.numpy as jnp
from jax import lax
import numpy as np

D_MODEL = 1024
BATCH = 8
SEQ = 4096
DEPTH = 2

D_HEAD = 64
GRID_W = 64
Q_BLOCK = 128
RMS_EPS = 1e-6
H_A = D_MODEL // (2 * D_HEAD)
WIN_R = 8
WIN_C = 16
H_B = D_MODEL // (2 * D_HEAD)
H_B_KV = H_B // 4
AXIAL_THETA = 10000.0
H_C = D_MODEL // (2 * D_HEAD)
Q_LORA = D_MODEL // 4
KV_LORA = D_MODEL // 8
C_NOPE = 64
C_ROPE = 32
C_V = 64
MLA_THETA = 10000.0
H_D = D_MODEL // (4 * D_HEAD)
D_V = 2 * D_HEAD
ROPE_THETA = 500000.0
ROT_DIM = D_HEAD // 4
GATE_E = (H_A + H_B) * D_HEAD
SPLIT_E = [H_A * D_HEAD, H_A * D_HEAD, H_A * D_HEAD,
           H_B * D_HEAD, H_B_KV * D_HEAD, H_B_KV * D_HEAD, GATE_E]
GATE_O = H_C * C_V + H_D * D_V
SPLIT_O = [Q_LORA, KV_LORA, C_ROPE, 2 * H_D * D_HEAD, 2 * H_D * D_HEAD,
           H_D * D_V, GATE_O]
IN_E = sum(SPLIT_E)
IN_O = sum(SPLIT_O)
N_EVEN = (DEPTH + 1) // 2
N_ODD = DEPTH // 2

kernel_name = "hybrid_natten_gqa_mla_diff_encoder"


def _split(x, sizes):
    idx = [int(v) for v in np.cumsum(sizes)[:-1]]
    return jnp.split(x, idx, axis=-1)


def rmsnorm(x, g):
    xf = x.astype(jnp.float32)
    y = xf * lax.rsqrt(jnp.mean(xf * xf, axis=-1, keepdims=True) + RMS_EPS)
    return (y * g.astype(jnp.float32)).astype(x.dtype)


def rope_angles(pos, dim, theta):
    inv = jnp.power(theta, -jnp.arange(0, dim, 2, dtype=jnp.float32) / dim)
    return pos[:, None] * inv[None, :]


def rotate(x, ang):
    shp = (ang.shape[0],) + (1,) * (x.ndim - 3) + (ang.shape[1],)
    c = jnp.cos(ang).reshape(shp).astype(x.dtype)
    s = jnp.sin(ang).reshape(shp).astype(x.dtype)
    x1, x2 = jnp.split(x, 2, axis=-1)
    return jnp.concatenate([x1 * c - x2 * s, x2 * c + x1 * s], axis=-1)


def attend_blocks(q, k, v, scale):
    B, S, Hq, dq = q.shape
    Hk, dv = k.shape[2], v.shape[3]
    G = Hq // Hk
    nb = S // Q_BLOCK
    qb = q.reshape(B, nb, Q_BLOCK, Hk, G, dq).transpose(1, 0, 2, 3, 4, 5)

    def one(qi):
        s = jnp.einsum('bqhgd,bkhd->bhgqk', qi, k).astype(jnp.float32) * scale
        p = jax.nn.softmax(s, axis=-1).astype(v.dtype)
        return jnp.einsum('bhgqk,bkhd->bqhgd', p, v)

    o = lax.map(one, qb)
    return o.transpose(1, 0, 2, 3, 4, 5).reshape(B, S, Hq, dv)


def diff_attend_blocks(q1, q2, k1, k2, v, lam, scale):
    B, S, H, d = q1.shape
    nb = S // Q_BLOCK
    q1b = q1.reshape(B, nb, Q_BLOCK, H, d).transpose(1, 0, 2, 3, 4)
    q2b = q2.reshape(B, nb, Q_BLOCK, H, d).transpose(1, 0, 2, 3, 4)

    def one(args):
        a, b = args
        s1 = jnp.einsum('bqhd,bkhd->bhqk', a, k1).astype(jnp.float32) * scale
        s2 = jnp.einsum('bqhd,bkhd->bhqk', b, k2).astype(jnp.float32) * scale
        p = jax.nn.softmax(s1, axis=-1) - lam * jax.nn.softmax(s2, axis=-1)
        return jnp.einsum('bhqk,bkhd->bqhd', p.astype(v.dtype), v)

    o = lax.map(one, (q1b, q2b))
    return o.transpose(1, 0, 2, 3, 4).reshape(B, S, H, v.shape[-1])


def neighbourhood_attention(q, k, v, rpb):
    B, S, H, d = q.shape
    rows = S // GRID_W
    wr = min(WIN_R, rows)
    wc = WIN_C
    qg = q.reshape(B, rows, GRID_W, H, d)
    kg = k.reshape(B, rows, GRID_W, H, d)
    vg = v.reshape(B, rows, GRID_W, H, d)
    col = jnp.arange(GRID_W)
    c0 = jnp.clip(col - wc // 2, 0, GRID_W - wc)
    cidx = c0[:, None] + jnp.arange(wc)[None, :]
    cb = (cidx - col[:, None] + (WIN_C - 1))[:, None, :]
    scale = d ** -0.5

    def one(r):
        r0 = jnp.clip(r - wr // 2, 0, rows - wr)
        kr = lax.dynamic_slice_in_dim(kg, r0, wr, axis=1)
        vr = lax.dynamic_slice_in_dim(vg, r0, wr, axis=1)
        kw = kr[:, :, cidx]
        vw = vr[:, :, cidx]
        qr = lax.dynamic_index_in_dim(qg, r, axis=1, keepdims=False)
        s = jnp.einsum('bqhd,biqjhd->bhqij', qr, kw).astype(jnp.float32) * scale
        rb = (r0 + jnp.arange(wr) - r + (WIN_R - 1))[None, :, None]
        bias = rpb[:, rb, cb].astype(jnp.float32)
        s = (s + bias[None]).reshape(B, H, GRID_W, wr * wc)
        p = jax.nn.softmax(s, axis=-1).reshape(B, H, GRID_W, wr, wc).astype(v.dtype)
        return jnp.einsum('bhqij,biqjhd->bqhd', p, vw)

    o = lax.map(one, jnp.arange(rows))
    return o.transpose(1, 0, 2, 3, 4).reshape(B, S, H, d)


def even_layer(x, norm, w_in, gq_a, gk_a, rpb, gq_b, gk_b, w_out, ang_row, ang_col):
    B, S, _ = x.shape
    h = rmsnorm(x, norm)
    qa, ka, va, qb, kb, vb, gate = _split(h @ w_in, SPLIT_E)
    qa = rmsnorm(qa.reshape(B, S, H_A, D_HEAD), gq_a)
    ka = rmsnorm(ka.reshape(B, S, H_A, D_HEAD), gk_a)
    va = va.reshape(B, S, H_A, D_HEAD)
    oa = neighbourhood_attention(qa, ka, va, rpb)
    half = D_HEAD // 2
    qb = rmsnorm(qb.reshape(B, S, H_B, D_HEAD), gq_b)
    kb = rmsnorm(kb.reshape(B, S, H_B_KV, D_HEAD), gk_b)
    qb = jnp.concatenate([rotate(qb[..., :half], ang_row), rotate(qb[..., half:], ang_col)], -1)
    kb = jnp.concatenate([rotate(kb[..., :half], ang_row), rotate(kb[..., half:], ang_col)], -1)
    vb = vb.reshape(B, S, H_B_KV, D_HEAD)
    ob = attend_blocks(qb, kb, vb, D_HEAD ** -0.5)
    o = jnp.concatenate([oa.reshape(B, S, -1), ob.reshape(B, S, -1)], -1) * jax.nn.silu(gate)
    return o @ w_out


def odd_layer(x, norm, w_in, g_cq, w_cq_b, g_ckv, w_ckv_b, gq_c, gk_c, gq_d, gk_d,
              lam_q1, lam_k1, lam_q2, lam_k2, g_sub_d, w_out, ang_mla, ang_part, lam_init):
    B, S, _ = x.shape
    h = rmsnorm(x, norm)
    cq, ckv, kpe, qd, kd, vd, gate = _split(h @ w_in, SPLIT_O)
    q = (rmsnorm(cq, g_cq) @ w_cq_b).reshape(B, S, H_C, C_NOPE + C_ROPE)
    kv = (rmsnorm(ckv, g_ckv) @ w_ckv_b).reshape(B, S, H_C, C_NOPE + C_V)
    k_nope, v_c = kv[..., :C_NOPE], kv[..., C_NOPE:]
    k_pe = jnp.broadcast_to(kpe[:, :, None, :], (B, S, H_C, C_ROPE))
    qc = rmsnorm(q, gq_c)
    kc = rmsnorm(jnp.concatenate([k_nope, k_pe], -1), gk_c)
    qc = jnp.concatenate([qc[..., :C_NOPE], rotate(qc[..., C_NOPE:], ang_mla)], -1)
    kc = jnp.concatenate([kc[..., :C_NOPE], rotate(kc[..., C_NOPE:], ang_mla)], -1)
    oc = attend_blocks(qc, kc, v_c, (C_NOPE + C_ROPE) ** -0.5)
    qd = rmsnorm(qd.reshape(B, S, H_D, 2, D_HEAD), gq_d)
    kd = rmsnorm(kd.reshape(B, S, H_D, 2, D_HEAD), gk_d)
    qd = jnp.concatenate([rotate(qd[..., :ROT_DIM], ang_part), qd[..., ROT_DIM:]], -1)
    kd = jnp.concatenate([rotate(kd[..., :ROT_DIM], ang_part), kd[..., ROT_DIM:]], -1)
    vd = vd.reshape(B, S, H_D, D_V)
    f32 = jnp.float32
    lam = (jnp.exp(jnp.sum(lam_q1.astype(f32) * lam_k1.astype(f32)))
           - jnp.exp(jnp.sum(lam_q2.astype(f32) * lam_k2.astype(f32))) + lam_init)
    od = diff_attend_blocks(qd[:, :, :, 0], qd[:, :, :, 1], kd[:, :, :, 0], kd[:, :, :, 1],
                            vd, lam, D_HEAD ** -0.5)
    od = rmsnorm(od, g_sub_d) * (1.0 - lam_init)
    o = jnp.concatenate([oc.reshape(B, S, -1), od.reshape(B, S, -1)], -1) * jax.nn.silu(gate)
    return o @ w_out


def setup_inputs(seed: int = 0) -> dict:
    key = jax.random.key(seed)
    ks = iter(jax.random.split(key, 32))
    f32 = jnp.float32

    def w(shape, fan_in):
        return jax.random.normal(next(ks), shape, f32) * (fan_in ** -0.5)

    def gain(shape):
        return 1.0 + 0.1 * jax.random.normal(next(ks), shape, f32)

    def small(shape, s):
        return s * jax.random.normal(next(ks), shape, f32)

    NE, NO = N_EVEN, N_ODD
    return {
        "x": jax.random.normal(next(ks), (BATCH, SEQ, D_MODEL), f32),
        "norm_e": gain((NE, D_MODEL)),
        "w_in_e": w((NE, D_MODEL, IN_E), D_MODEL),
        "gq_a": gain((NE, D_HEAD)),
        "gk_a": gain((NE, D_HEAD)),
        "rpb_a": small((NE, H_A, 2 * WIN_R - 1, 2 * WIN_C - 1), 0.2),
        "gq_b": gain((NE, D_HEAD)),
        "gk_b": gain((NE, D_HEAD)),
        "w_out_e": w((NE, GATE_E, D_MODEL), GATE_E),
        "norm_o": gain((NO, D_MODEL)),
        "w_in_o": w((NO, D_MODEL, IN_O), D_MODEL),
        "g_cq": gain((NO, Q_LORA)),
        "w_cq_b": w((NO, Q_LORA, H_C * (C_NOPE + C_ROPE)), Q_LORA),
        "g_ckv": gain((NO, KV_LORA)),
        "w_ckv_b": w((NO, KV_LORA, H_C * (C_NOPE + C_V)), KV_LORA),
        "gq_c": gain((NO, C_NOPE + C_ROPE)),
        "gk_c": gain((NO, C_NOPE + C_ROPE)),
        "gq_d": gain((NO, D_HEAD)),
        "gk_d": gain((NO, D_HEAD)),
        "lam_q1": small((NO, D_HEAD), 0.1),
        "lam_k1": small((NO, D_HEAD), 0.1),
        "lam_q2": small((NO, D_HEAD), 0.1),
        "lam_k2": small((NO, D_HEAD), 0.1),
        "g_sub_d": gain((NO, D_V)),
        "w_out_o": w((NO, GATE_O, D_MODEL), GATE_O),
    }


def reference(x, norm_e, w_in_e, gq_a, gk_a, rpb_a, gq_b, gk_b, w_out_e,
              norm_o, w_in_o, g_cq, w_cq_b, g_ckv, w_ckv_b, gq_c, gk_c, gq_d, gk_d,
              lam_q1, lam_k1, lam_q2, lam_k2, g_sub_d, w_out_o):
    S = x.shape[1]
    t = jnp.arange(S)
    pos = t.astype(jnp.float32)
    row = (t // GRID_W).astype(jnp.float32)
    col = (t % GRID_W).astype(jnp.float32)
    half = D_HEAD // 2
    ang_row = rope_angles(row, half, AXIAL_THETA)
    ang_col = rope_angles(col, half, AXIAL_THETA)
    ang_mla = rope_angles(pos, C_ROPE, MLA_THETA)
    ang_part = rope_angles(pos, ROT_DIM, ROPE_THETA)
    for l in range(DEPTH):
        i = l // 2
        if l % 2 == 0:
            x = x + even_layer(x, norm_e[i], w_in_e[i], gq_a[i], gk_a[i], rpb_a[i],
                               gq_b[i], gk_b[i], w_out_e[i], ang_row, ang_col)
        else:
            lam_init = 0.8 - 0.6 * math.exp(-0.3 * l)
            x = x + odd_layer(x, norm_o[i], w_in_o[i], g_cq[i], w_cq_b[i], g_ckv[i],
                              w_ckv_b[i], gq_c[i], gk_c[i], gq_d[i], gk_d[i],
                              lam_q1[i], lam_k1[i], lam_q2[i], lam_k2[i], g_sub_d[i],
                              w_out_o[i], ang_mla, ang_part, lam_init)
    return x
```

```python
import math
from contextlib import ExitStack
import numpy as np
import concourse.bass as bass
import concourse.mybir as mybir
from concourse.bass_utils import run_bass_kernel_spmd

F32 = mybir.dt.float32
BF16 = mybir.dt.bfloat16
AF = mybir.ActivationFunctionType
ALU = mybir.AluOpType
AX = mybir.AxisListType

SEQ = 4096
DM = 1024
NBLK = 8
EPS = 1e-6
LAM_INIT = 0.8 - 0.6 * math.exp(-0.3 * 1)
MASKV = -30000.0

COMPUTE = ("pe", "act", "dve", "pool")
SEM_ROLL = 3000
NDMA_SEM = 8


class Res:
    __slots__ = ("name", "w", "rs")

    def __init__(self, name=""):
        self.name = name
        self.w = None
        self.rs = []


class Op:
    __slots__ = ("q", "fn", "dma", "waits", "signal", "sem", "val")

    def __init__(self, q, fn, dma):
        self.q = q
        self.fn = fn
        self.dma = dma
        self.waits = []
        self.signal = False
        self.sem = None
        self.val = None


class Sched:
    def __init__(self):
        self.queues = {k: [] for k in ("pe", "act", "dve", "pool", "sp")}
        self.order = []

    def op(self, q, fn, reads=(), writes=(), dma=False):
        o = Op(q, fn, dma)
        deps = []
        for r in reads:
            if r.w is not None:
                deps.append(r.w)
        for r in writes:
            if r.w is not None:
                deps.append(r.w)
            deps.extend(r.rs)
        seen = set()
        for d in deps:
            if d is o or id(d) in seen:
                continue
            seen.add(id(d))
            if not d.dma and not o.dma and d.q == o.q:
                if o.q == "pe":
                    continue
                if not any((r.w is d) for r in reads):
                    continue
            o.waits.append(d)
        for r in reads:
            r.rs.append(o)
        for r in writes:
            r.w = o
            r.rs = []
        self.queues[q].append(o)
        self.order.append(o)
        return o

    def finalize(self, nc, stack):
        for o in self.order:
            for d in o.waits:
                d.signal = True
        self.sems = {}
        for q in COMPUTE:
            cnt = 0
            for o in self.queues[q]:
                if o.dma or not o.signal:
                    continue
                key = (q, cnt // SEM_ROLL)
                if key not in self.sems:
                    self.sems[key] = stack.enter_context(nc.semaphore(f"s_{q}_{key[1]}"))
                o.sem = self.sems[key]
                o.val = cnt % SEM_ROLL + 1
                cnt += 1
        for q, lst in self.queues.items():
            k = 0
            hist = []
            for o in lst:
                if not o.dma:
                    continue
                j = k % NDMA_SEM
                key = ("dma", q, j)
                if key not in self.sems:
                    self.sems[key] = stack.enter_context(nc.semaphore(f"d_{q}_{j}"))
                o.sem = self.sems[key]
                o.val = 16 * (k // NDMA_SEM + 1)
                o.signal = True
                if k >= NDMA_SEM:
                    o.waits.append(hist[k - NDMA_SEM])
                hist.append(o)
                k += 1

    def emit(self, q, eng):
        seen = {}
        for o in self.queues[q]:
            need = {}
            for d in o.waits:
                key = id(d.sem)
                if seen.get(key, 0) >= d.val:
                    continue
                if key not in need or need[key][1] < d.val:
                    need[key] = (d.sem, d.val)
            for key, (sem, val) in need.items():
                eng.wait_ge(sem, val)
                seen[key] = val
            ins = o.fn(eng)
            if o.signal:
                ins.then_inc(o.sem, 16 if o.dma else 1)

    def run(self, nc, stack, final_waits=()):
        self.finalize(nc, stack)
        block = stack.enter_context(nc.Block())
        S = self

        @block.tensor
        def _(e):
            S.emit("pe", e)

        @block.scalar
        def _(e):
            S.emit("act", e)

        @block.vector
        def _(e):
            S.emit("dve", e)

        @block.gpsimd
        def _(e):
            S.emit("pool", e)
            for o in final_waits:
                if o.q == "pool":
                    e.wait_ge(o.sem, o.val)

        @block.sync
        def _(e):
            S.emit("sp", e)
            for o in final_waits:
                if o.q == "sp":
                    e.wait_ge(o.sem, o.val)


C_ID, C_BLK64, C_ONES96, C_ONES, C_RB, C_RC, C_RD, C_SWAP = range(8)
GOFF = 8 * 128
G_QA, G_KA, G_QB, G_KB, G_CKV, G_QC, G_KC, G_QD, G_KD, G_SUB, G_CQ0, G_CQ1, G_NE, G_NO = 0, 1, 2, 3, 4, 5, 6, 7, 8, 9, 10, 11, 12, 20
NCST = GOFF + 28


def _rot_lhsT(pairs):
    R = np.zeros((128, 128), np.float32)
    for a, b, n in pairs:
        for i in range(n):
            R[a + i, b + i] = -1.0
            R[b + i, a + i] = 1.0
    return np.ascontiguousarray(R.T)


def _angles(pos, dim, theta):
    inv = np.power(np.float32(theta), -np.arange(0, dim, 2, dtype=np.float32) / np.float32(dim)).astype(np.float32)
    return (pos.astype(np.float32)[:, None] * inv[None, :]).astype(np.float32)


def host_constants(inp):
    f = np.float32
    cst = np.zeros((128, NCST), f)
    cst[:, 0:128] = np.eye(128, dtype=f)
    blk = np.zeros((128, 128), f)
    blk[0:64, 0:64] = 1
    blk[64:128, 64:128] = 1
    cst[:, 128:256] = blk
    o96 = np.zeros((128, 128), f)
    o96[0:96, 0:96] = 1
    cst[:, 256:384] = o96
    cst[:, 384:512] = 1.0
    rb = []
    for h0 in (0, 64):
        rb += [(h0 + 0, h0 + 16, 16), (h0 + 32, h0 + 48, 16)]
    cst[:, 512:640] = _rot_lhsT(rb)
    cst[:, 640:768] = _rot_lhsT([(64, 80, 16)])
    cst[:, 768:896] = _rot_lhsT([(0, 8, 8), (64, 72, 8)])
    cst[:, 896:1024] = np.roll(np.eye(128, dtype=f), 64, axis=0)
    g = cst[:, GOFF:]
    t2 = lambda v: np.concatenate([v, v]).astype(f)
    pad = lambda v: np.concatenate([v, np.zeros(128 - v.shape[0], f)]).astype(f)
    g[:, G_QA] = t2(inp["gq_a"][0])
    g[:, G_KA] = t2(inp["gk_a"][0])
    g[:, G_QB] = t2(inp["gq_b"][0])
    g[:, G_KB] = t2(inp["gk_b"][0])
    g[:, G_CKV] = inp["g_ckv"][0]
    g[:, G_QC] = pad(inp["gq_c"][0])
    g[:, G_KC] = pad(inp["gk_c"][0])
    g[:, G_QD] = t2(inp["gq_d"][0])
    g[:, G_KD] = t2(inp["gk_d"][0])
    g[:, G_SUB] = inp["g_sub_d"][0]
    g[:, G_CQ0] = inp["g_cq"][0][0:128]
    g[:, G_CQ1] = inp["g_cq"][0][128:256]
    g[:, G_NE:G_NE + 8] = inp["norm_e"][0].reshape(8, 128).T
    g[:, G_NO:G_NO + 8] = inp["norm_o"][0].reshape(8, 128).T
    t = np.arange(SEQ)
    row = (t // 64).astype(f)
    col = (t % 64).astype(f)
    pos = t.astype(f)
    a_row = _angles(row, 32, 10000.0)
    a_col = _angles(col, 32, 10000.0)
    a_mla = _angles(pos, 32, 10000.0)
    a_par = _angles(pos, 16, 500000.0)
    tabs = np.zeros((6, 128, SEQ), f)
    tabs[0::2] = 1.0
    for h0 in (0, 64):
        for (o, a) in ((0, a_row), (32, a_col)):
            c = np.cos(a).astype(f).T
            s = np.sin(a).astype(f).T
            tabs[0, h0 + o:h0 + o + 16] = c
            tabs[0, h0 + o + 16:h0 + o + 32] = c
            tabs[1, h0 + o:h0 + o + 16] = s
            tabs[1, h0 + o + 16:h0 + o + 32] = s
        c = np.cos(a_par).astype(f).T
        s = np.sin(a_par).astype(f).T
        tabs[4, h0:h0 + 8] = c
        tabs[4, h0 + 8:h0 + 16] = c
        tabs[5, h0:h0 + 8] = s
        tabs[5, h0 + 8:h0 + 16] = s
    c = np.cos(a_mla).astype(f).T
    s = np.sin(a_mla).astype(f).T
    tabs[2, 64:80] = c
    tabs[2, 80:96] = c
    tabs[3, 64:80] = s
    tabs[3, 80:96] = s
    rpb = inp["rpb_a"][0]
    biasA = np.full((8, 5, 128, 640), MASKV, f)
    for ci, tq in enumerate(A_CLASS_REP):
        kt0 = a_kt0(tq)
        qr = 2 * tq + np.arange(128) // 64
        qc = np.arange(128) % 64
        r0 = np.clip(qr - 4, 0, 56)
        c0 = np.clip(qc - 8, 0, 48)
        for slot in range(5):
            kt = kt0 + slot
            kr = 2 * kt + np.arange(128) // 64
            kc = np.arange(128) % 64
            dr = kr[:, None] - qr[None, :]
            dc = kc[:, None] - qc[None, :]
            valid = ((kr[:, None] >= r0[None, :]) & (kr[:, None] < r0[None, :] + 8)
                     & (kc[:, None] >= c0[None, :]) & (kc[:, None] < c0[None, :] + 16))
            vals = rpb[:, np.clip(dr + 7, 0, 14), np.clip(dc + 15, 0, 30)]
            biasA[:, ci, :, slot * 128:(slot + 1) * 128] = np.where(valid[None], vals, f(MASKV))
    lamv = np.concatenate([inp["lam_q1"][0], inp["lam_k1"][0], inp["lam_q2"][0], inp["lam_k2"][0]])[None, :].astype(f)
    return cst, tabs, biasA, lamv


A_CLASS_REP = (0, 1, 2, 30, 31)


def a_kt0(t):
    return min(max(t - 2, 0), 27)


def a_class(t):
    return {0: 0, 1: 1, 30: 3, 31: 4}.get(t, 2)


class Bld:
    def __init__(self, layers, x_from_out=False):
        self.layers = layers
        nc = self.nc = bass.Bass("TRN2", target_bir_lowering=False)
        self.S = Sched()
        self.rot = {}
        d = lambda n, s, k="ExternalInput": nc.dram_tensor(n, s, F32, kind=k).ap()
        self.x = d("x", [SEQ, DM])
        self.out = d("out", [SEQ, DM], "ExternalOutput")
        self.w_in_e = d("w_in_e", [DM, 3328])
        self.w_out_e = d("w_out_e", [DM, DM])
        self.w_in_o = d("w_in_o", [DM, 2976])
        self.w_cq_b = d("w_cq_b", [256, 768])
        self.w_ckv_b = d("w_ckv_b", [128, 1024])
        self.w_out_o = d("w_out_o", [DM, DM])
        self.cst_d = d("cst", [128, NCST])
        self.tabs_d = d("tabs", [6, 128, SEQ])
        self.biasA_d = d("biasA", [8, 5, 128, 640])
        self.lamv_d = d("lamv", [1, 256])

    def nxt(self, name, n):
        i = self.rot.get(name, 0)
        self.rot[name] = i + 1
        return i % n

    def alloc(self, st):
        nc = self.nc
        sb = lambda n, s, t: st.enter_context(nc.sbuf_tensor("sb_" + n, s, t))
        ps = lambda n, s: st.enter_context(nc.psum_tensor("ps_" + n, s, F32))
        self.cst = sb("cst", [128, NCST], F32)
        self.identb = sb("identb", [128, 128], BF16)
        self.onesb = sb("onesb", [128, 128], BF16)
        self.LV = sb("LV", [128, 256], F32)
        self.lamc = sb("lamc", [128, 8], F32)
        self.stat = sb("stat", [128, 8], F32)
        self.hT = sb("hT", [128, 8, SEQ], BF16)
        self.oT = sb("oT", [128, 8, SEQ], BF16)
        self.PB = sb("PB", [128, 10240], BF16)
        self.KT = self.PB[:, 0:4096]
        self.V = self.PB[:, 4096:8192].rearrange("p (t c) -> p t c", c=128)
        self.VA = self.PB[:, 4096:10240].rearrange("p (t c) -> p t c", c=192)
        self.WoB = self.PB[:, 0:8192].rearrange("p (c n) -> p c n", c=8)
        self.QTb = sb("QTb", [128, 2, 512], BF16)
        self.PT = sb("PT", [128, 3, 1024], BF16)
        self.wb = sb("wb", [128, 4, 1024], BF16)
        self.wst = sb("wst", [128, 2, 1024], F32)
        self.SCR = sb("SCR", [128, 10, 512], F32)
        self.sbA = sb("sbA", [128, 2, 640], F32)
        self.wpe_t = sb("wpe_t", [128, 8, 32], BF16)
        self.R_wpe = Res("wpe")
        self.biasT = self.oT[:, 6:8, :].rearrange("p c t -> p (c t)")[:, 0:5120].bitcast(F32).rearrange(
            "p (b h n) -> p b h n", b=2, h=2)
        self.Sps = [ps("S0", [128, 1024]), ps("S1", [128, 1024])]
        self.bank = {i: ps(f"B{i}", [128, 512]) for i in (4, 5, 6, 7)}
        R = lambda n: Res(n)
        self.R_cst = R("cst")
        self.R_lam = R("lam")
        self.R_hT = [R(f"hT{b}") for b in range(8)]
        self.R_oT = [[R(f"oT{c}_{b}") for b in range(8)] for c in range(8)]
        self.R_KT = [R(f"KT{b}") for b in range(8)]
        self.R_V = [R(f"V{b}") for b in range(8)]
        self.R_QT = [R("QT0"), R("QT1")]
        self.R_PT = [R(f"PT{i}") for i in range(3)]
        self.R_wb = [R(f"wb{i}") for i in range(4)]
        self.R_wst = [R(f"wst{i}") for i in range(2)]
        self.R_scr = [R(f"scr{i}") for i in range(10)]
        self.R_sbA = [R("sbA0"), R("sbA1")]
        self.R_bias = [R("bias0"), R("bias1")]
        self.R_S = [R("S0"), R("S1")]
        self.R_bank = {i: R(f"B{i}") for i in (4, 5, 6, 7)}
        self.R_stat = [R(f"stat{i}") for i in range(8)]
        self.R_out = [R(f"out{t}") for t in range(32)]
        self.free_banks = [6, 7]
        self.free_scr = list(range(10))

    def cmat(self, k, n=128):
        return self.cst[0:n, k * 128:k * 128 + n]

    def gcol(self, k, n=128):
        return self.cst[0:n, GOFF + k:GOFF + k + 1]

    def scr(self):
        i = self.free_scr.pop(0)
        return self.SCR[:, i, :], self.R_scr[i]

    def rel(self, *rs):
        for r in rs:
            i = self.R_scr.index(r)
            assert i not in self.free_scr
            self.free_scr.append(i)

    def set_misc(self, lst):
        self.free_banks = list(lst)

    def mbank(self):
        i = self.free_banks.pop(0)
        return self.bank[i], self.R_bank[i]

    def relb(self, *rs):
        for r in rs:
            i = [k for k, v in self.R_bank.items() if v is r][0]
            assert i not in self.free_banks
            self.free_banks.append(i)

    def mm(self, out, lhsT, rhs, start, stop, reads, writes, tp=None):
        if tp is None:
            fn = lambda e: e.matmul(out, lhsT=lhsT, rhs=rhs, start=start, stop=stop)
        else:
            fn = lambda e: e.matmul(out, lhsT=lhsT, rhs=rhs, start=start, stop=stop, tile_position=tp)
        return self.S.op("pe", fn, reads=reads, writes=writes)

    def act(self, out, in_, func, reads, writes, scale=1.0, bias=None):
        if bias is None:
            fn = lambda e: e.activation(out=out, in_=in_, func=func, scale=scale)
        else:
            fn = lambda e: e.activation(out=out, in_=in_, func=func, scale=scale, bias=bias)
        return self.S.op("act", fn, reads=reads, writes=writes)

    def tt(self, q, out, in0, in1, op, reads, writes):
        return self.S.op(q, lambda e: e.tensor_tensor(out=out, in0=in0, in1=in1, op=op), reads=reads, writes=writes)

    def stt(self, q, out, in0, scalar, in1, op0, op1, reads, writes):
        return self.S.op(q, lambda e: e.scalar_tensor_tensor(out=out, in0=in0, scalar=scalar, in1=in1, op0=op0, op1=op1),
                         reads=reads, writes=writes)

    def ts(self, q, out, in0, s1, op0, reads, writes, s2=None, op1=None):
        if op1 is None:
            fn = lambda e: e.tensor_scalar(out=out, in0=in0, scalar1=s1, scalar2=None, op0=op0)
        else:
            fn = lambda e: e.tensor_scalar(out=out, in0=in0, scalar1=s1, scalar2=s2, op0=op0, op1=op1)
        return self.S.op(q, fn, reads=reads, writes=writes)

    def recip(self, out, in_, reads, writes):
        return self.S.op("dve", lambda e: e.reciprocal(out=out, in_=in_), reads=reads, writes=writes)

    def copy(self, q, out, in_, reads, writes):
        if q == "act":
            return self.act(out, in_, AF.Copy, reads, writes)
        return self.S.op(q, lambda e: e.tensor_copy(out=out, in_=in_), reads=reads, writes=writes)

    def dma(self, q, out, in_, reads, writes):
        return self.S.op(q, lambda e: e.dma_start(out=out, in_=in_), reads=reads, writes=writes, dma=True)

    def setup(self):
        self.dma("sp", self.cst[:], self.cst_d, [], [self.R_cst])
        self.dma("sp", self.LV[:], self.lamv_d.partition_broadcast(128), [], [self.R_lam])
        self.copy("dve", self.identb[:], self.cmat(C_ID), [self.R_cst], [self.R_cst])
        self.copy("dve", self.onesb[:], self.cmat(C_ONES), [self.R_cst], [self.R_cst])
        if 1 in self.layers:
            lc, LV = self.lamc, self.LV
            tmp, rt = self.scr()
            for j in range(2):
                self.tt("dve", tmp[:, 0:64], LV[:, 128 * j:128 * j + 64], LV[:, 128 * j + 64:128 * j + 128], ALU.mult,
                        [self.R_lam], [rt])
                self.S.op("dve", lambda e, j=j: e.reduce_sum(out=lc[:, j:j + 1], in_=tmp[:, 0:64], axis=AX.X),
                          reads=[rt], writes=[self.R_lam])
                self.act(lc[:, 2 + j:3 + j], lc[:, j:j + 1], AF.Exp, [self.R_lam], [self.R_lam])
            self.tt("dve", lc[:, 4:5], lc[:, 3:4], lc[:, 2:3], ALU.subtract, [self.R_lam], [self.R_lam])
            self.ts("dve", lc[:, 4:5], lc[:, 4:5], -LAM_INIT, ALU.add, [self.R_lam], [self.R_lam])
            self.ts("dve", lc[:, 5:6], self.gcol(G_SUB), 1.0 - LAM_INIT, ALU.mult, [self.R_lam, self.R_cst], [self.R_lam])
            self.rel(rt)

    def load_w(self, w, specs, M, nk=8, fold=None, dst=None):
        si = self.nxt("wst", 2)
        stage = self.wst[:, si, :].rearrange("p (c m) -> p c m", c=8)
        if dst is None:
            wi = self.nxt("wb", 4)
            wbv = self.wb[:, wi, :].rearrange("p (c m) -> p c m", c=8)
            rw = self.R_wb[wi]
        else:
            wbv, rw = dst
        wv = w.rearrange("(c p) n -> p c n", p=128)
        for (off, src, n) in specs:
            self.dma("sp", stage[:, 0:nk, off:off + n], wv[:, :, src:src + n], [], [self.R_wst[si]])
        for c in range(nk):
            if fold is not None:
                self.ts("dve", wbv[:, c, 0:M], stage[:, c, 0:M], self.gcol(fold + c), ALU.mult,
                        [self.R_wst[si], self.R_cst], [rw])
            else:
                self.copy("dve", wbv[:, c, 0:M], stage[:, c, 0:M], [self.R_wst[si]], [rw])
        return wbv, rw

    def xnorm_tile(self, t, xt, r_xt):
        i = t % 2
        sq = self.SCR[:, 6 + 2 * i:8 + 2 * i, :].rearrange("p a b -> p (a b)")
        r_sq = [self.R_scr[6 + 2 * i], self.R_scr[7 + 2 * i]]
        self.act(sq, xt, AF.Square, r_xt, r_sq)
        st = self.stat[:, i:i + 1]
        rs = self.R_stat[i]
        self.S.op("dve", lambda e: e.reduce_sum(out=st, in_=sq, axis=AX.X), reads=r_sq, writes=[rs])
        self.act(st, st, AF.Ln, [rs], [rs], scale=1.0 / DM, bias=EPS)
        self.act(st, st, AF.Exp, [rs], [rs], scale=-0.5)
        pi = self.nxt("PT", 3)
        xs = self.PT[:, pi, :]
        self.ts("dve", xs, xt, st, ALU.mult, r_xt + [rs], [self.R_PT[pi]])
        bkb = self.Sps[i][:, 0:512].bitcast(BF16)
        rb = self.R_S[i]
        for c in range(8):
            self.S.op("pe", lambda e, c=c: e.transpose(bkb[:, c * 128:(c + 1) * 128], xs[:, c * 128:(c + 1) * 128], self.identb[:]),
                      reads=[self.R_PT[pi], self.R_cst], writes=[rb])
        self.copy("act", self.hT[:, :, t * 128:(t + 1) * 128], bkb.rearrange("p (c k) -> p c k", c=8),
                  [rb], [self.R_hT[t // 4]])

    def xt_view(self, t, nbuf=3):
        i = t % nbuf
        return (self.SCR[:, 2 * i:2 * i + 2, :].rearrange("p a b -> p (a b)"), [self.R_scr[2 * i], self.R_scr[2 * i + 1]])

    def prologue_from_x(self):
        for t in range(32):
            xt, r_xt = self.xt_view(t)
            self.dma("sp", xt, self.x[t * 128:(t + 1) * 128, :], [], r_xt)
            self.xnorm_tile(t, xt, r_xt)

    def out_proj(self, w_out, src, last, next_norm):
        rPB = self.R_KT + self.R_V
        for c in range(8):
            si = self.nxt("wst", 2)
            stage = self.wst[:, si, :]
            self.dma("sp", stage, w_out[c * 128:(c + 1) * 128, :], [], [self.R_wst[si]])
            self.copy("pool", self.WoB[:, c, :], stage, [self.R_wst[si]], rPB)
        stores = []
        nbuf = 3 if next_norm else 4
        for t in range(32):
            xt, r_xt = self.xt_view(t, nbuf)
            if src is self.out:
                self.dma("sp", xt, src[t * 128:(t + 1) * 128, :], [self.R_out[t]], r_xt)
            else:
                self.dma("sp", xt, src[t * 128:(t + 1) * 128, :], [], r_xt)
            pair = (4, 5) if t % 2 == 0 else (6, 7)
            for n in range(2):
                bk, rb = self.bank[pair[n]], self.R_bank[pair[n]]
                for c in range(8):
                    self.mm(bk[:], self.oT[:, c, t * 128:(t + 1) * 128], self.WoB[:, c, n * 512:(n + 1) * 512],
                            c == 0, c == 7, [self.R_oT[c][t // 4]] + rPB, [rb])
                self.tt("dve", xt[:, n * 512:(n + 1) * 512], bk[:], xt[:, n * 512:(n + 1) * 512], ALU.add,
                        [rb] + r_xt, r_xt)
            stores.append(self.dma("pool", self.out[t * 128:(t + 1) * 128, :], xt, r_xt, [self.R_out[t]]))
            if next_norm:
                self.xnorm_tile(t, xt, r_xt)
        return stores

    def fm_proj(self, bk, rb, wbv, rw, M, blk, nk=8, rows0=0, rhs=None, rrhs=None, first=True, last=True):
        tp = (0, rows0) if rows0 else None
        for c in range(nk):
            r = self.hT[:, c, blk * 512:(blk + 1) * 512] if rhs is None else rhs(c)
            rr = [self.R_hT[blk]] if rhs is None else rrhs
            self.mm(bk[rows0:rows0 + M, :], wbv[:, c, 0:M], r, first and c == 0, last and c == nk - 1, [rw] + rr, [rb], tp)

    def norm_chain_g(self, proj, n, ones_k, inv_d, gk, rot, blk, out_ap, out_res, lowbank=False):
        bk, rb = self.mbank()
        proj(bk, rb)
        yield
        sq, rsq = self.scr()
        self.act(sq[0:n], bk[0:n, :], AF.Square, [rb], [rsq])
        if lowbank:
            pc, rpc = self.scr()
            self.act(pc[0:n], bk[0:n, :], AF.Copy, [rb], [rpc])
            self.relb(rb)
            src, rsrc = pc[0:n], rpc
        else:
            src, rsrc = bk[0:n, :], rb
        yield
        b2, rb2 = self.mbank()
        self.mm(b2[0:n, :], self.cmat(ones_k, n), sq[0:n], True, True, [rsq, self.R_cst], [rb2])
        if rot is not None:
            rk, ti = rot
            tcs, rtc = self.scr()
            tss, rts = self.scr()
            self.dma("sp", tcs[0:n], self.tabs_d[ti, 0:n, blk * 512:(blk + 1) * 512], [], [rtc])
            self.dma("sp", tss[0:n], self.tabs_d[ti + 1, 0:n, blk * 512:(blk + 1) * 512], [], [rts])
        yield
        self.act(sq[0:n], b2[0:n, :], AF.Ln, [rb2], [rsq], scale=inv_d, bias=EPS)
        self.relb(rb2)
        self.act(sq[0:n], sq[0:n], AF.Exp, [rsq], [rsq], scale=-0.5)
        if rot is None:
            self.stt("dve", out_ap, src, self.gcol(gk, n), sq[0:n], ALU.mult, ALU.mult,
                     [rsrc, rsq, self.R_cst], out_res)
            self.rel(rsq)
            if lowbank:
                self.rel(rpc)
            else:
                self.relb(rb)
            return
        kn, rkn = self.scr()
        self.stt("dve", kn[0:n], src, self.gcol(gk, n), sq[0:n], ALU.mult, ALU.mult,
                 [rsrc, rsq, self.R_cst], [rkn])
        self.rel(rsq)
        if lowbank:
            self.rel(rpc)
        else:
            self.relb(rb)
        yield
        b3, rb3 = self.mbank()
        self.mm(b3[0:n, :], self.cmat(rk, n), kn[0:n], True, True, [rkn, self.R_cst], [rb3])
        yield
        self.tt("pool", kn[0:n], kn[0:n], tcs[0:n], ALU.mult, [rkn, rtc], [rkn])
        self.tt("dve", tss[0:n], b3[0:n, :], tss[0:n], ALU.mult, [rb3, rts], [rts])
        self.relb(rb3)
        self.tt("pool", out_ap, kn[0:n], tss[0:n], ALU.add, [rkn, rts], out_res)
        self.rel(rkn, rtc, rts)

    @staticmethod
    def drain(gen):
        for _ in gen:
            pass

    @staticmethod
    def staggered(gens, every=2):
        active = []
        for g in gens:
            active.append(g)
            for _ in range(every):
                for a in list(active):
                    try:
                        next(a)
                    except StopIteration:
                        active.remove(a)
        while active:
            for a in list(active):
                try:
                    next(a)
                except StopIteration:
                    active.remove(a)

    def gate_g(self, wg, rwg, M, p0, blk, state, act_recip=False):
        bk, rb = self.mbank()
        self.fm_proj(bk, rb, wg, rwg, M, blk, rows0=p0)
        sl = slice(p0, p0 + M)
        yield
        e, re_ = self.scr()
        gs, rgs = self.scr()
        self.act(e[sl], bk[sl, :], AF.Exp, [rb], [re_], scale=-1.0)
        self.act(gs[sl], bk[sl, :], AF.Copy, [rb], [rgs])
        self.relb(rb)
        state[("g", blk)] = (gs, rgs)
        yield
        if act_recip:
            self.act(e[sl], e[sl], AF.Ln, [re_], [re_], scale=1.0, bias=1.0)
            self.act(e[sl], e[sl], AF.Exp, [re_], [re_], scale=-1.0)
        else:
            self.ts("dve", e[sl], e[sl], 1.0, ALU.add, [re_], [re_])
            self.recip(e[sl], e[sl], [re_], [re_])
        self.tt("pool", gs[sl], gs[sl], e[sl], ALU.mult, [rgs, re_], [rgs])
        self.rel(re_)

    def gate_stage(self, wg, rwg, M, p0, blk):
        st = {}
        self.drain(self.gate_g(wg, rwg, M, p0, blk, st))
        return st[("g", blk)]

    def epilogue_ab(self, p0, M, chunk, blk, gs, rgs, extra_w=()):
        sl = slice(p0, p0 + M)
        cO, rcO = self.scr()
        cS, rcS = self.scr()
        self.act(cS[sl], self.bank[5][sl, :], AF.Ln, [self.R_bank[5]], [rcS])
        self.act(cO[sl], self.bank[4][sl, :], AF.Copy, [self.R_bank[4]], [rcO])
        self.act(cS[sl], cS[sl], AF.Exp, [rcS], [rcS], scale=-1.0)
        self.tt("pool", cO[sl], cO[sl], cS[sl], ALU.mult, [rcO, rcS], [rcO])
        self.tt("pool", self.oT[sl, chunk, blk * 512:(blk + 1) * 512], cO[sl], gs[sl], ALU.mult,
                [rcO, rgs], [self.R_oT[chunk][blk]] + list(extra_w))
        self.rel(rcO, rcS, rgs)

    def epilogue_merged(self, parts, p0, M, krows, chunk, blk, gs, rgs, extra_w=()):
        sl = slice(p0, p0 + M)
        cO, rcO = self.scr()
        cS, rcS = self.scr()
        for (bi, (ol, oh), (s0, s1)) in parts:
            self.act(cS[s0:s1], self.bank[bi][s0:s1, :], AF.Copy, [self.R_bank[bi]], [rcS])
            self.act(cO[ol:oh], self.bank[bi][ol:oh, :], AF.Copy, [self.R_bank[bi]], [rcO])
        bm, rbm = self.mbank()
        k0, k1 = krows
        self.mm(bm[:], self.cst[k0:k1, C_SWAP * 128:(C_SWAP + 1) * 128], cS[k0:k1], True, True, [rcS, self.R_cst], [rbm])
        self.recip(cS[sl], bm[sl, :], [rbm], [rcS])
        self.relb(rbm)
        self.tt("pool", cO[sl], cO[sl], cS[sl], ALU.mult, [rcO, rcS], [rcO])
        self.tt("pool", self.oT[sl, chunk, blk * 512:(blk + 1) * 512], cO[sl], gs[sl], ALU.mult,
                [rcO, rgs], [self.R_oT[chunk][blk]] + list(extra_w))
        self.rel(rcO, rcS, rgs)

    def prime1(self, streams):
        s = streams[0]
        lo, hi = s["kt"]
        for u in range(2):
            for c in range(2):
                kc = 2 * u + c
                self.mm(self.Sps[u][:, c * 512:(c + 1) * 512], self.KT[lo:hi, kc * 128:(kc + 1) * 128], s["q"],
                        True, True, [self.R_KT[kc // 4], s["rq"]], [self.R_S[u]])

    def prime2(self, streams):
        for u in range(2):
            for j, s in enumerate(streams):
                lo, hi = s["kt"]
                self.mm(self.Sps[u][:, j * 512:(j + 1) * 512], self.KT[lo:hi, u * 128:(u + 1) * 128], s["q"],
                        True, True, [self.R_KT[u // 4], s["rq"]], [self.R_S[u]])

    def attn_block(self, streams, scale, hooks, primed=False):
        units = [(kg, j) for kg in range(16) for j in range(len(streams))]
        nu = len(units)

        def qk(u):
            kg, j = units[u]
            s = streams[j]
            lo, hi = s["kt"]
            si = u % 2
            for c in range(2):
                kc = 2 * kg + c
                self.mm(self.Sps[si][:, c * 512:(c + 1) * 512], self.KT[lo:hi, kc * 128:(kc + 1) * 128], s["q"],
                        True, True, [self.R_KT[kc // 4], s["rq"]], [self.R_S[si]])

        def rest(u):
            kg, j = units[u]
            s = streams[j]
            si = u % 2
            pi = self.nxt("PT", 3)
            self.act(self.PT[:, pi, :], self.Sps[si][:], AF.Exp, [self.R_S[si]], [self.R_PT[pi]], scale=scale)
            for c in range(2):
                kc = 2 * kg + c
                for (out_ap, lhs_fn, res, tp, rv) in s["pv"]:
                    self.mm(out_ap, lhs_fn(kc), self.PT[:, pi, c * 512:(c + 1) * 512], kc == 0, kc == 31,
                            [self.R_PT[pi]] + rv(kc), [res], tp)

        assert len(streams) == 1
        if not primed:
            qk(0)
            qk(1)
        for u in range(nu):
            rest(u)
            if u + 2 < nu:
                qk(u + 2)
            if u in hooks:
                hooks[u]()

    def attn_block2(self, streams, scale, hooks, primed=False):
        nu = 32
        npv = len(streams[0]["pv"])

        def qk(u):
            si = u % 2
            for j, s in enumerate(streams):
                lo, hi = s["kt"]
                self.mm(self.Sps[si][:, j * 512:(j + 1) * 512], self.KT[lo:hi, u * 128:(u + 1) * 128], s["q"],
                        True, True, [self.R_KT[u // 4], s["rq"]], [self.R_S[si]])

        def rest(u):
            si = u % 2
            pi = self.nxt("PT", 3)
            self.act(self.PT[:, pi, :], self.Sps[si][:], AF.Exp, [self.R_S[si]], [self.R_PT[pi]], scale=scale)
            for k in range(npv):
                for j, s in enumerate(streams):
                    (out_ap, lhs_fn, res, tp, rv) = s["pv"][k]
                    self.mm(out_ap, lhs_fn(u), self.PT[:, pi, j * 512:(j + 1) * 512], u == 0, u == nu - 1,
                            [self.R_PT[pi]] + rv(u), [res], tp)

        if not primed:
            qk(0)
            qk(1)
        for u in range(nu):
            rest(u)
            if u + 2 < nu:
                qk(u + 2)
            if u in hooks:
                hooks[u]()

    def kchain(self, w, rw, n_rows_proj, ones_k, inv_d, gk, rot, blk, M=128):
        proj = lambda bk, rb: self.fm_proj(bk, rb, w, rw, M, blk)
        return self.norm_chain_g(proj, M, ones_k, inv_d, gk, rot, blk,
                                 self.KT[0:M, blk * 512:(blk + 1) * 512], [self.R_KT[blk]])

    def qchain(self, w, rw, ones_k, inv_d, gk, rot, blk, state, M=128, nk=8, rhs=None, rrhs=None, lowbank=False):
        qi = self.nxt("QT", 2)
        state[("q", blk)] = qi
        proj = lambda bk, rb: self.fm_proj(bk, rb, w, rw, M, blk, nk=nk, rhs=rhs, rrhs=rrhs)
        return self.norm_chain_g(proj, M, ones_k, inv_d, gk, rot, blk, self.QTb[0:M, qi, :], [self.R_QT[qi]], lowbank=lowbank)

    @staticmethod
    def step(gen):
        try:
            next(gen)
        except StopIteration:
            pass

    def layer0(self):
        W = self.w_in_e
        for p in range(4):
            wk, rwk = self.load_w(W, [(0, 512 + 128 * p, 128)], 128, fold=G_NE)
            wv, rwv = self.load_w(W, [(0, 1024 + 128 * p, 128)], 128, fold=G_NE)
            wq, rwq = self.load_w(W, [(0, 128 * p, 128)], 128, fold=G_NE)
            wg, rwg = self.load_w(W, [(0, 2304 + 128 * p, 128)], 128, fold=G_NE)
            self.load_biasA(p, 2, 0)
            self.set_misc([4, 5, 6, 7])
            state = {}
            self.staggered([self.kchain(wk, rwk, 128, C_BLK64, 1.0 / 64, G_KA, None, blk) for blk in range(8)]
                           + [self.qchain(wq, rwq, C_BLK64, 1.0 / 64, G_QA, None, 0, state),
                              self.gate_g(wg, rwg, 128, 0, 0, state, act_recip=True)])
            for blk in range(8):
                self.v_block(wv, rwv, blk, 128)
            self.set_misc([6, 7])
            cur = {"cls": None}
            info = {}
            gates = {}

            def qk(t):
                blk, tl = divmod(t, 4)
                qi = state[("q", blk)]
                kt0 = a_kt0(t)
                for slot in range(5):
                    kt = kt0 + slot
                    for j in range(2):
                        lo, hi = 64 * j, 64 * j + 64
                        self.mm(self.Sps[j][:, slot * 128:(slot + 1) * 128], self.KT[lo:hi, kt * 128:(kt + 1) * 128],
                                self.QTb[lo:hi, qi, tl * 128:(tl + 1) * 128], True, True,
                                [self.R_KT[kt // 4], self.R_QT[qi]], [self.R_S[j]])

            def soft(t):
                cls = a_class(t)
                if cls == 2:
                    bi = 0
                else:
                    bi = 1
                    if cur["cls"] != cls:
                        self.load_biasA(p, cls, 1)
                        cur["cls"] = cls
                pis = []
                for j in range(2):
                    ai = self.nxt("sbA", 2)
                    self.stt("dve", self.sbA[:, ai, :], self.Sps[j][:, 0:640], 0.125, self.biasT[:, bi, j, :],
                             ALU.mult, ALU.add, [self.R_S[j], self.R_bias[bi]], [self.R_sbA[ai]])
                    pi = self.nxt("PT", 3)
                    self.act(self.PT[:, pi, 0:640], self.sbA[:, ai, :], AF.Exp, [self.R_sbA[ai]], [self.R_PT[pi]])
                    pis.append(pi)
                info[t] = pis

            def pv(t):
                blk, tl = divmod(t, 4)
                kt0 = a_kt0(t)
                pis = info[t]
                for slot in range(5):
                    kt = kt0 + slot
                    for bank in (4, 5):
                        for j in range(2):
                            lo, hi = 64 * j, 64 * j + 64
                            tp = (0, lo) if lo else None
                            rhs = self.PT[:, pis[j], slot * 128:(slot + 1) * 128]
                            if bank == 4:
                                lhsT, rr = self.V[:, kt, lo:hi], [self.R_V[kt // 4]]
                            else:
                                lhsT, rr = self.onesb[:, 0:64], [self.R_cst]
                            self.mm(self.bank[bank][lo:hi, tl * 128:(tl + 1) * 128], lhsT, rhs, slot == 0, slot == 4,
                                    [self.R_PT[pis[j]]] + rr, [self.R_bank[bank]], tp)

            nxtq = self.qchain(wq, rwq, C_BLK64, 1.0 / 64, G_QA, None, 1, state)
            nxtg = self.gate_g(wg, rwg, 128, 0, 1, state, act_recip=True)
            qk(0)
            for t in range(32):
                blk, tl = divmod(t, 4)
                soft(t)
                if t + 1 < 32:
                    if tl == 3:
                        self.drain(nxtq)
                    qk(t + 1)
                pv(t)
                if tl < 3:
                    self.step(nxtq)
                    self.step(nxtg)
                else:
                    self.drain(nxtg)
                    self.epilogue_ab(0, 128, p, blk, *state[("g", blk)])
                    nxtq = self.qchain(wq, rwq, C_BLK64, 1.0 / 64, G_QA, None, blk + 2, state) if blk + 2 < 8 else iter(())
                    nxtg = self.gate_g(wg, rwg, 128, 0, blk + 2, state, act_recip=True) if blk + 2 < 8 else iter(())
        self.va_ones()
        for g in range(2):
            wv, rwv = self.load_w(W, [(0, 2176 + 64 * g, 64)], 64, fold=G_NE)
            for blk in range(8):
                self.v_block(wv, rwv, blk, 64, dst=self.VA[:, :, 64:128])
            wk, rwk = self.load_w(W, [(0, 2048 + 64 * g, 64), (64, 2048 + 64 * g, 64)], 128, fold=G_NE)
            self.set_misc([4, 5, 6, 7])
            self.staggered([self.kchain(wk, rwk, 128, C_BLK64, 1.0 / 64, G_KB, (C_RB, 0), blk) for blk in range(8)])
            self.set_misc([6, 7])
            for sp_ in range(2):
                cq = 2 * g + sp_
                chunk = 4 + cq
                wq, rwq = self.load_w(W, [(0, 1536 + 128 * cq, 128)], 128, fold=G_NE)
                wg, rwg = self.load_w(W, [(0, 2304 + 128 * chunk, 128)], 128, fold=G_NE)
                state = {}
                extra = self.R_bias if chunk >= 6 else ()
                self.drain(self.qchain(wq, rwq, C_BLK64, 1.0 / 64, G_QB, (C_RB, 0), 0, state))
                def mk(blk):
                    qi = state[("q", blk)]
                    streams = []
                    for j in range(2):
                        lo, hi = 64 * j, 64 * j + 64
                        c0 = 64 if j == 0 else 0
                        streams.append(dict(
                            kt=(lo, hi), q=self.QTb[lo:hi, qi, :], rq=self.R_QT[qi],
                            pv=[(self.bank[4 + j][:], (lambda kc, c0=c0: self.VA[:, kc, c0:c0 + 128]), self.R_bank[4 + j], None,
                                 (lambda kc: [self.R_V[kc // 4]]))]))
                    return streams

                self.prime2(mk(0))
                for blk in range(8):
                    streams = mk(blk)
                    nxtq = self.qchain(wq, rwq, C_BLK64, 1.0 / 64, G_QB, (C_RB, 0), blk + 1, state) if blk + 1 < 8 else iter(())

                    gg = self.gate_g(wg, rwg, 128, 0, blk, state)
                    hooks = {}
                    for u in (2, 4, 6):
                        hooks[u] = (lambda gg=gg: self.step(gg))
                    for u in (9, 11, 13, 15, 17, 19):
                        hooks[u] = (lambda nq=nxtq: self.step(nq))
                    self.attn_block2(streams, 0.125, hooks, primed=True)
                    self.drain(gg)
                    self.drain(nxtq)
                    if blk + 1 < 8:
                        self.prime2(mk(blk + 1))
                    gs, rgs = state[("g", blk)]
                    self.epilogue_merged([(4, (0, 64), (64, 128)), (5, (64, 128), (0, 64))], 0, 128, (0, 128),
                                         chunk, blk, gs, rgs, extra)

    def load_biasA(self, p, cls, bi):
        for j in range(2):
            self.dma("sp", self.biasT[:, bi, j, :], self.biasA_d[2 * p + j, cls], [], [self.R_bias[bi]])

    def v_block(self, wv, rwv, blk, ncol, rhs_nk=8, lhs=None, rl=None, dst=None):
        bk, rb = self.mbank()
        for tl in range(4):
            t = blk * 4 + tl
            for c in range(rhs_nk):
                l = self.hT[:, c, t * 128:(t + 1) * 128] if lhs is None else lhs(t)
                rr = [self.R_hT[blk]] if lhs is None else rl(blk)
                self.mm(bk[:, tl * ncol:(tl + 1) * ncol], l, wv[:, c, 0:ncol], c == 0, c == rhs_nk - 1, [rwv] + rr, [rb])
        dv = self.V[:, blk * 4:(blk + 1) * 4, 0:ncol] if dst is None else dst[:, blk * 4:(blk + 1) * 4, :]
        self.copy("act", dv, bk[:, 0:4 * ncol].rearrange("p (t c) -> p t c", t=4), [rb], [self.R_V[blk]])
        self.relb(rb)

    def va_ones(self):
        for sl in (slice(0, 64), slice(128, 192)):
            self.S.op("pool", lambda e, sl=sl: e.memset(self.VA[:, :, sl], 1.0), reads=[], writes=self.R_V)

    def layer1(self):
        W = self.w_in_o
        CQN = [self.oT[:, 5, :], self.oT[:, 6, :]]
        CKVN = self.oT[:, 7, :]
        r_cqn = lambda blk: [self.R_oT[5][blk], self.R_oT[6][blk]]
        r_ckvn = lambda blk: [self.R_oT[7][blk]]
        wc0, rwc0 = self.load_w(W, [(0, 0, 128)], 128, fold=G_NO)
        wc1, rwc1 = self.load_w(W, [(0, 128, 128)], 128, fold=G_NO)
        wkv, rwkv = self.load_w(W, [(0, 256, 128)], 128, fold=G_NO)
        for blk in range(8):
            b6, r6 = self.bank[6], self.R_bank[6]
            b7, r7 = self.bank[7], self.R_bank[7]
            b5, r5 = self.bank[5], self.R_bank[5]
            self.fm_proj(b6, r6, wc0, rwc0, 128, blk)
            self.fm_proj(b7, r7, wc1, rwc1, 128, blk)
            s0, rs0 = self.scr()
            s1, rs1 = self.scr()
            self.act(s0, b6[:], AF.Square, [r6], [rs0])
            self.act(s1, b7[:], AF.Square, [r7], [rs1])
            self.mm(b5[:], self.cmat(C_ONES), s0, True, False, [rs0, self.R_cst], [r5])
            self.mm(b5[:], self.cmat(C_ONES), s1, False, True, [rs1, self.R_cst], [r5])
            self.act(s0, b5[:], AF.Ln, [r5], [rs0], scale=1.0 / 256, bias=EPS)
            self.act(s0, s0, AF.Exp, [rs0], [rs0], scale=-0.5)
            sl = slice(blk * 512, (blk + 1) * 512)
            self.stt("dve", CQN[0][:, sl], b6[:], self.gcol(G_CQ0), s0, ALU.mult, ALU.mult, [r6, rs0, self.R_cst], [self.R_oT[5][blk]])
            self.stt("dve", CQN[1][:, sl], b7[:], self.gcol(G_CQ1), s0, ALU.mult, ALU.mult, [r7, rs0, self.R_cst], [self.R_oT[6][blk]])
            self.rel(rs0, rs1)
        self.set_misc([4, 5, 6, 7])
        self.staggered([self.norm_chain_g((lambda bk, rb, blk=blk: self.fm_proj(bk, rb, wkv, rwkv, 128, blk)),
                                          128, C_ONES, 1.0 / 128, G_CKV, None, blk,
                                          CKVN[:, blk * 512:(blk + 1) * 512], [self.R_oT[7][blk]]) for blk in range(8)])
        self.va_ones()
        wpe, rwpe = self.load_w(W, [(0, 384, 32)], 32, fold=G_NO, dst=(self.wpe_t, self.R_wpe))
        for h in range(8):
            half = 64 * (h % 2)
            chunk = h // 2
            wkn, rwkn = self.load_w(self.w_ckv_b, [(0, 128 * h, 128)], 128, nk=1)
            wqb, rwqb = self.load_w(self.w_cq_b, [(0, 96 * h, 96)], 96, nk=2)
            wg, rwg = self.load_w(W, [(0, 1952 + 64 * h, 64)], 64, fold=G_NO)
            self.set_misc([4, 5, 6, 7])

            def kproj(bk, rb, blk, wkn=wkn, rwkn=rwkn):
                sl = slice(blk * 512, (blk + 1) * 512)
                self.mm(bk[0:64, :], wkn[:, 0, 0:64], CKVN[:, sl], True, True, [rwkn] + r_ckvn(blk), [rb])
                self.fm_proj(bk, rb, wpe, rwpe, 32, blk, rows0=64)

            state = {}

            def qch(blk, wqb=wqb, rwqb=rwqb):
                return self.qchain(wqb, rwqb, C_ONES96, 1.0 / 96, G_QC, (C_RC, 2), blk, state, M=96, nk=2,
                                   rhs=(lambda c: CQN[c][:, blk * 512:(blk + 1) * 512]), rrhs=r_cqn(blk))

            self.staggered([self.norm_chain_g((lambda bk, rb, blk=blk: kproj(bk, rb, blk)), 96, C_ONES96, 1.0 / 96, G_KC,
                                              (C_RC, 2), blk, self.KT[0:96, blk * 512:(blk + 1) * 512], [self.R_KT[blk]])
                            for blk in range(8)] + [qch(0)])
            for blk in range(8):
                self.v_block(wkn[:, :, 64:128], rwkn, blk, 64, rhs_nk=1,
                             lhs=(lambda t: CKVN[:, t * 128:(t + 1) * 128]), rl=r_ckvn, dst=self.VA[:, :, 64:128])
            self.set_misc([5, 6, 7])
            c0 = 64 if half == 0 else 0
            other = (64, 128) if half == 0 else (0, 64)
            def mk(blk, c0=c0):
                qi = state[("q", blk)]
                return [dict(
                    kt=(0, 96), q=self.QTb[0:96, qi, :], rq=self.R_QT[qi],
                    pv=[(self.bank[4][:], (lambda kc, c0=c0: self.VA[:, kc, c0:c0 + 128]), self.R_bank[4], None,
                         (lambda kc: [self.R_V[kc // 4]]))])]

            self.prime1(mk(0))
            for blk in range(8):
                streams = mk(blk)
                nxtq = qch(blk + 1) if blk + 1 < 8 else iter(())

                gg = self.gate_g(wg, rwg, 64, half, blk, state)
                hooks = {}
                for u in (1, 2, 3):
                    hooks[u] = (lambda gg=gg: self.step(gg))
                for u in (5, 6, 7, 8, 9, 10):
                    hooks[u] = (lambda nq=nxtq: self.step(nq))
                self.attn_block(streams, 96 ** -0.5, hooks, primed=True)
                self.drain(gg)
                self.drain(nxtq)
                if blk + 1 < 8:
                    self.prime1(mk(blk + 1))
                gs, rgs = state[("g", blk)]
                self.epilogue_merged([(4, (half, half + 64), other)], half, 64, other, chunk, blk, gs, rgs)
        for h in range(4):
            chunk = 4 + h
            wk, rwk = self.load_w(W, [(0, 928 + 128 * h, 128)], 128, fold=G_NO)
            wv, rwv = self.load_w(W, [(0, 1440 + 128 * h, 128)], 128, fold=G_NO)
            wq, rwq = self.load_w(W, [(0, 416 + 128 * h, 128)], 128, fold=G_NO)
            wg, rwg = self.load_w(W, [(0, 1952 + 512 + 128 * h, 128)], 128, fold=G_NO)
            self.set_misc([4, 5, 6, 7])
            state = {}
            self.staggered([self.kchain(wk, rwk, 128, C_BLK64, 1.0 / 64, G_KD, (C_RD, 4), blk) for blk in range(8)]
                           + [self.qchain(wq, rwq, C_BLK64, 1.0 / 64, G_QD, (C_RD, 4), 0, state),
                              self.gate_g(wg, rwg, 128, 0, 0, state)])
            for blk in range(8):
                self.v_block(wv, rwv, blk, 128)
            self.set_misc([7])
            prev_epi = None
            def mk(blk):
                qi = state[("q", blk)]
                streams = []
                for j in range(2):
                    lo, hi = 64 * j, 64 * j + 64
                    tp = (0, lo) if lo else None
                    streams.append(dict(
                        kt=(lo, hi), q=self.QTb[lo:hi, qi, :], rq=self.R_QT[qi],
                        pv=[(self.bank[4 + j][:], (lambda kc: self.V[:, kc, :]), self.R_bank[4 + j], None,
                             (lambda kc: [self.R_V[kc // 4]])),
                            (self.bank[6][lo:hi, :], (lambda kc: self.onesb[:, 0:64]), self.R_bank[6], tp,
                             (lambda kc: [self.R_cst]))]))
                return streams

            self.prime2(mk(0))
            for blk in range(8):
                streams = mk(blk)
                nxtq = (self.qchain(wq, rwq, C_BLK64, 1.0 / 64, G_QD, (C_RD, 4), blk + 1, state, lowbank=True)
                        if blk + 1 < 8 else iter(()))
                nxtg = self.gate_g(wg, rwg, 128, 0, blk + 1, state) if blk + 1 < 8 else iter(())
                hooks = {}
                if prev_epi is not None:
                    for u in (6, 7, 8):
                        hooks[u] = (lambda ep=prev_epi: self.step(ep))
                for u in (9, 11, 13):
                    hooks[u] = (lambda gg=nxtg: self.step(gg))
                for u in (15, 17, 19, 21, 23, 25):
                    hooks[u] = (lambda nq=nxtq: self.step(nq))
                self.attn_block2(streams, 0.125, hooks, primed=True)
                if prev_epi is not None:
                    self.drain(prev_epi)
                self.drain(nxtg)
                self.drain(nxtq)
                if blk + 1 < 8:
                    self.prime2(mk(blk + 1))
                prev_epi = self.d_epilogue_g(chunk, blk, *state[("g", blk)])
                self.step(prev_epi)
            self.drain(prev_epi)

    def d_epilogue_g(self, chunk, blk, gs, rgs):
        c1, rc1 = self.scr()
        c2, rc2 = self.scr()
        c3, rc3 = self.scr()
        self.act(c3, self.bank[6][:], AF.Copy, [self.R_bank[6]], [rc3])
        self.act(c1, self.bank[4][:], AF.Copy, [self.R_bank[4]], [rc1])
        self.act(c2, self.bank[5][:], AF.Copy, [self.R_bank[5]], [rc2])
        self.recip(c3, c3, [rc3], [rc3])
        bsw, rsw = self.mbank()
        self.mm(bsw[:], self.cmat(C_SWAP), c3, True, True, [rc3, self.R_cst], [rsw])
        self.tt("pool", c1[0:64], c1[0:64], c3[0:64], ALU.mult, [rc1, rc3], [rc1])
        self.tt("dve", c1[64:128], c1[64:128], bsw[64:128, :], ALU.mult, [rc1, rsw], [rc1])
        self.tt("dve", c2[0:64], c2[0:64], bsw[0:64, :], ALU.mult, [rc2, rsw], [rc2])
        self.relb(rsw)
        self.tt("pool", c2[64:128], c2[64:128], c3[64:128], ALU.mult, [rc2, rc3], [rc2])
        self.stt("dve", c1, c2, self.lamc[:, 4:5], c1, ALU.mult, ALU.add, [rc1, rc2, self.R_lam], [rc1])
        self.rel(rc3)
        yield
        self.act(c2, c1, AF.Square, [rc1], [rc2])
        yield
        bq, rq = self.mbank()
        self.mm(bq[:], self.cmat(C_ONES), c2, True, True, [rc2, self.R_cst], [rq])
        yield
        self.act(c2, bq[:], AF.Ln, [rq], [rc2], scale=1.0 / 128, bias=EPS)
        self.relb(rq)
        self.act(c2, c2, AF.Exp, [rc2], [rc2], scale=-0.5)
        self.stt("dve", c1, c1, self.lamc[:, 5:6], c2, ALU.mult, ALU.mult, [rc1, rc2, self.R_lam], [rc1])
        self.tt("pool", self.oT[:, chunk, blk * 512:(blk + 1) * 512], c1, gs, ALU.mult,
                [rc1, rgs], [self.R_oT[chunk][blk]])
        self.rel(rc1, rc2, rgs)

    def build(self):
        with ExitStack() as st:
            self.alloc(st)
            self.setup()
            self.prologue_from_x()
            stores = []
            if 0 in self.layers:
                self.layer0()
                stores = self.out_proj(self.w_out_e, self.x, last=(1 not in self.layers), next_norm=(1 in self.layers))
                src = self.out
            else:
                src = self.x
            if 1 in self.layers:
                self.layer1()
                stores = self.out_proj(self.w_out_o, src, last=True, next_norm=False)
            self.S.run(self.nc, st, final_waits=stores)
        return self.nc


_CACHE = {}


def _program(layers):
    if layers not in _CACHE:
        _CACHE[layers] = Bld(layers).build()
    return _CACHE[layers]


def kernel(**inputs):
    inp = {k: np.ascontiguousarray(np.asarray(v, dtype=np.float32)) for k, v in inputs.items()}
    cst, tabs, biasA, lamv = host_constants(inp)
    shared = {
        "w_in_e": inp["w_in_e"][0], "w_out_e": inp["w_out_e"][0], "w_in_o": inp["w_in_o"][0],
        "w_cq_b": inp["w_cq_b"][0], "w_ckv_b": inp["w_ckv_b"][0], "w_out_o": inp["w_out_o"][0],
        "cst": cst, "tabs": tabs, "biasA": biasA, "lamv": lamv,
    }
    x = inp["x"]
    nb = x.shape[0]
    nc = _program((0, 1))
    in_maps = [dict(shared, x=np.ascontiguousarray(x[b])) for b in range(nb)]
    res = run_bass_kernel_spmd(nc, in_maps, core_ids=list(range(nb)))
    return np.stack([np.asarray(r["out"], dtype=np.float32) for r in res.results], axis=0)
```

```python
import math
from contextlib import ExitStack
import numpy as np
import concourse.bass as bass
import concourse.mybir as mybir
from concourse.bass_utils import run_bass_kernel_spmd

F32 = mybir.dt.float32
BF16 = mybir.dt.bfloat16
AF = mybir.ActivationFunctionType
ALU = mybir.AluOpType
AX = mybir.AxisListType

SEQ = 4096
DM = 1024
NBLK = 8
EPS = 1e-6
LAM_INIT = 0.8 - 0.6 * math.exp(-0.3 * 1)
MASKV = -30000.0

COMPUTE = ("pe", "act", "dve", "pool")
SEM_ROLL = 3000
NDMA_SEM = 8


class Res:
    __slots__ = ("name", "w", "rs")

    def __init__(self, name=""):
        self.name = name
        self.w = None
        self.rs = []


class Op:
    __slots__ = ("q", "fn", "dma", "waits", "signal", "sem", "val")

    def __init__(self, q, fn, dma):
        self.q = q
        self.fn = fn
        self.dma = dma
        self.waits = []
        self.signal = False
        self.sem = None
        self.val = None


class Sched:
    def __init__(self):
        self.queues = {k: [] for k in ("pe", "act", "dve", "pool", "sp")}
        self.order = []

    def op(self, q, fn, reads=(), writes=(), dma=False):
        o = Op(q, fn, dma)
        deps = []
        for r in reads:
            if r.w is not None:
                deps.append(r.w)
        for r in writes:
            if r.w is not None:
                deps.append(r.w)
            deps.extend(r.rs)
        seen = set()
        for d in deps:
            if d is o or id(d) in seen:
                continue
            seen.add(id(d))
            if not d.dma and not o.dma and d.q == o.q:
                if o.q == "pe":
                    continue
                if not any((r.w is d) for r in reads):
                    continue
            o.waits.append(d)
        for r in reads:
            r.rs.append(o)
        for r in writes:
            r.w = o
            r.rs = []
        self.queues[q].append(o)
        self.order.append(o)
        return o

    def finalize(self, nc, stack):
        for o in self.order:
            for d in o.waits:
                d.signal = True
        self.sems = {}
        for q in COMPUTE:
            cnt = 0
            for o in self.queues[q]:
                if o.dma or not o.signal:
                    continue
                key = (q, cnt // SEM_ROLL)
                if key not in self.sems:
                    self.sems[key] = stack.enter_context(nc.semaphore(f"s_{q}_{key[1]}"))
                o.sem = self.sems[key]
                o.val = cnt % SEM_ROLL + 1
                cnt += 1
        for q, lst in self.queues.items():
            k = 0
            hist = []
            for o in lst:
                if not o.dma:
                    continue
                j = k % NDMA_SEM
                key = ("dma", q, j)
                if key not in self.sems:
                    self.sems[key] = stack.enter_context(nc.semaphore(f"d_{q}_{j}"))
                o.sem = self.sems[key]
                o.val = 16 * (k // NDMA_SEM + 1)
                o.signal = True
                if k >= NDMA_SEM:
                    o.waits.append(hist[k - NDMA_SEM])
                hist.append(o)
                k += 1

    def emit(self, q, eng):
        seen = {}
        for o in self.queues[q]:
            need = {}
            for d in o.waits:
                key = id(d.sem)
                if seen.get(key, 0) >= d.val:
                    continue
                if key not in need or need[key][1] < d.val:
                    need[key] = (d.sem, d.val)
            for key, (sem, val) in need.items():
                eng.wait_ge(sem, val)
                seen[key] = val
            ins = o.fn(eng)
            if o.signal:
                ins.then_inc(o.sem, 16 if o.dma else 1)

    def run(self, nc, stack, final_waits=()):
        self.finalize(nc, stack)
        block = stack.enter_context(nc.Block())
        S = self

        @block.tensor
        def _(e):
            S.emit("pe", e)

        @block.scalar
        def _(e):
            S.emit("act", e)

        @block.vector
        def _(e):
            S.emit("dve", e)

        @block.gpsimd
        def _(e):
            S.emit("pool", e)
            for o in final_waits:
                if o.q == "pool":
                    e.wait_ge(o.sem, o.val)

        @block.sync
        def _(e):
            S.emit("sp", e)
            for o in final_waits:
                if o.q == "sp":
                    e.wait_ge(o.sem, o.val)


C_ID, C_BLK64, C_ONES96, C_ONES, C_RB, C_RC, C_RD, C_SWAP = range(8)
GOFF = 8 * 128
G_QA, G_KA, G_QB, G_KB, G_CKV, G_QC, G_KC, G_QD, G_KD, G_SUB, G_CQ0, G_CQ1, G_NE, G_NO = 0, 1, 2, 3, 4, 5, 6, 7, 8, 9, 10, 11, 12, 20
NCST = GOFF + 28


def _rot_lhsT(pairs):
    R = np.zeros((128, 128), np.float32)
    for a, b, n in pairs:
        for i in range(n):
            R[a + i, b + i] = -1.0
            R[b + i, a + i] = 1.0
    return np.ascontiguousarray(R.T)


def _angles(pos, dim, theta):
    inv = np.power(np.float32(theta), -np.arange(0, dim, 2, dtype=np.float32) / np.float32(dim)).astype(np.float32)
    return (pos.astype(np.float32)[:, None] * inv[None, :]).astype(np.float32)


def host_constants(inp):
    f = np.float32
    cst = np.zeros((128, NCST), f)
    cst[:, 0:128] = np.eye(128, dtype=f)
    blk = np.zeros((128, 128), f)
    blk[0:64, 0:64] = 1
    blk[64:128, 64:128] = 1
    cst[:, 128:256] = blk
    o96 = np.zeros((128, 128), f)
    o96[0:96, 0:96] = 1
    cst[:, 256:384] = o96
    cst[:, 384:512] = 1.0
    rb = []
    for h0 in (0, 64):
        rb += [(h0 + 0, h0 + 16, 16), (h0 + 32, h0 + 48, 16)]
    cst[:, 512:640] = _rot_lhsT(rb)
    cst[:, 640:768] = _rot_lhsT([(64, 80, 16)])
    cst[:, 768:896] = _rot_lhsT([(0, 8, 8), (64, 72, 8)])
    cst[:, 896:1024] = np.roll(np.eye(128, dtype=f), 64, axis=0)
    g = cst[:, GOFF:]
    t2 = lambda v: np.concatenate([v, v]).astype(f)
    pad = lambda v: np.concatenate([v, np.zeros(128 - v.shape[0], f)]).astype(f)
    g[:, G_QA] = t2(inp["gq_a"][0])
    g[:, G_KA] = t2(inp["gk_a"][0])
    g[:, G_QB] = t2(inp["gq_b"][0])
    g[:, G_KB] = t2(inp["gk_b"][0])
    g[:, G_CKV] = inp["g_ckv"][0]
    g[:, G_QC] = pad(inp["gq_c"][0])
    g[:, G_KC] = pad(inp["gk_c"][0])
    g[:, G_QD] = t2(inp["gq_d"][0])
    g[:, G_KD] = t2(inp["gk_d"][0])
    g[:, G_SUB] = inp["g_sub_d"][0]
    g[:, G_CQ0] = inp["g_cq"][0][0:128]
    g[:, G_CQ1] = inp["g_cq"][0][128:256]
    g[:, G_NE:G_NE + 8] = inp["norm_e"][0].reshape(8, 128).T
    g[:, G_NO:G_NO + 8] = inp["norm_o"][0].reshape(8, 128).T
    t = np.arange(SEQ)
    row = (t // 64).astype(f)
    col = (t % 64).astype(f)
    pos = t.astype(f)
    a_row = _angles(row, 32, 10000.0)
    a_col = _angles(col, 32, 10000.0)
    a_mla = _angles(pos, 32, 10000.0)
    a_par = _angles(pos, 16, 500000.0)
    tabs = np.zeros((6, 128, SEQ), f)
    tabs[0::2] = 1.0
    for h0 in (0, 64):
        for (o, a) in ((0, a_row), (32, a_col)):
            c = np.cos(a).astype(f).T
            s = np.sin(a).astype(f).T
            tabs[0, h0 + o:h0 + o + 16] = c
            tabs[0, h0 + o + 16:h0 + o + 32] = c
            tabs[1, h0 + o:h0 + o + 16] = s
            tabs[1, h0 + o + 16:h0 + o + 32] = s
        c = np.cos(a_par).astype(f).T
        s = np.sin(a_par).astype(f).T
        tabs[4, h0:h0 + 8] = c
        tabs[4, h0 + 8:h0 + 16] = c
        tabs[5, h0:h0 + 8] = s
        tabs[5, h0 + 8:h0 + 16] = s
    c = np.cos(a_mla).astype(f).T
    s = np.sin(a_mla).astype(f).T
    tabs[2, 64:80] = c
    tabs[2, 80:96] = c
    tabs[3, 64:80] = s
    tabs[3, 80:96] = s
    rpb = inp["rpb_a"][0]
    biasA = np.full((8, 5, 128, 640), MASKV, f)
    for ci, tq in enumerate(A_CLASS_REP):
        kt0 = a_kt0(tq)
        qr = 2 * tq + np.arange(128) // 64
        qc = np.arange(128) % 64
        r0 = np.clip(qr - 4, 0, 56)
        c0 = np.clip(qc - 8, 0, 48)
        for slot in range(5):
            kt = kt0 + slot
            kr = 2 * kt + np.arange(128) // 64
            kc = np.arange(128) % 64
            dr = kr[:, None] - qr[None, :]
            dc = kc[:, None] - qc[None, :]
            valid = ((kr[:, None] >= r0[None, :]) & (kr[:, None] < r0[None, :] + 8)
                     & (kc[:, None] >= c0[None, :]) & (kc[:, None] < c0[None, :] + 16))
            vals = rpb[:, np.clip(dr + 7, 0, 14), np.clip(dc + 15, 0, 30)]
            biasA[:, ci, :, slot * 128:(slot + 1) * 128] = np.where(valid[None], vals, f(MASKV))
    lamv = np.concatenate([inp["lam_q1"][0], inp["lam_k1"][0], inp["lam_q2"][0], inp["lam_k2"][0]])[None, :].astype(f)
    return cst, tabs, biasA, lamv


A_CLASS_REP = (0, 1, 2, 30, 31)


def a_kt0(t):
    return min(max(t - 2, 0), 27)


def a_class(t):
    return {0: 0, 1: 1, 30: 3, 31: 4}.get(t, 2)


class Bld:
    def __init__(self, layers, x_from_out=False):
        self.layers = layers
        nc = self.nc = bass.Bass("TRN2", target_bir_lowering=False)
        self.S = Sched()
        self.rot = {}
        d = lambda n, s, k="ExternalInput": nc.dram_tensor(n, s, F32, kind=k).ap()
        self.x = d("x", [SEQ, DM])
        self.out = d("out", [SEQ, DM], "ExternalOutput")
        self.w_in_e = d("w_in_e", [DM, 3328])
        self.w_out_e = d("w_out_e", [DM, DM])
        self.w_in_o = d("w_in_o", [DM, 2976])
        self.w_cq_b = d("w_cq_b", [256, 768])
        self.w_ckv_b = d("w_ckv_b", [128, 1024])
        self.w_out_o = d("w_out_o", [DM, DM])
        self.cst_d = d("cst", [128, NCST])
        self.tabs_d = d("tabs", [6, 128, SEQ])
        self.biasA_d = d("biasA", [8, 5, 128, 640])
        self.lamv_d = d("lamv", [1, 256])

    def nxt(self, name, n):
        i = self.rot.get(name, 0)
        self.rot[name] = i + 1
        return i % n

    def alloc(self, st):
        nc = self.nc
        sb = lambda n, s, t: st.enter_context(nc.sbuf_tensor("sb_" + n, s, t))
        ps = lambda n, s: st.enter_context(nc.psum_tensor("ps_" + n, s, F32))
        self.cst = sb("cst", [128, NCST], F32)
        self.identb = sb("identb", [128, 128], BF16)
        self.onesb = sb("onesb", [128, 128], BF16)
        self.LV = sb("LV", [128, 256], F32)
        self.lamc = sb("lamc", [128, 8], F32)
        self.stat = sb("stat", [128, 8], F32)
        self.hT = sb("hT", [128, 8, SEQ], BF16)
        self.oT = sb("oT", [128, 8, SEQ], BF16)
        self.PB = sb("PB", [128, 10240], BF16)
        self.KT = self.PB[:, 0:4096]
        self.V = self.PB[:, 4096:8192].rearrange("p (t c) -> p t c", c=128)
        self.VA = self.PB[:, 4096:10240].rearrange("p (t c) -> p t c", c=192)
        self.WoB = self.PB[:, 0:8192].rearrange("p (c n) -> p c n", c=8)
        self.QTb = sb("QTb", [128, 2, 512], BF16)
        self.PT = sb("PT", [128, 3, 1024], BF16)
        self.wb = sb("wb", [128, 4, 1024], BF16)
        self.wst = sb("wst", [128, 2, 1024], F32)
        self.SCR = sb("SCR", [128, 10, 512], F32)
        self.sbA = sb("sbA", [128, 2, 640], F32)
        self.wpe_t = sb("wpe_t", [128, 8, 32], BF16)
        self.R_wpe = Res("wpe")
        self.biasT = self.oT[:, 6:8, :].rearrange("p c t -> p (c t)")[:, 0:5120].bitcast(F32).rearrange(
            "p (b h n) -> p b h n", b=2, h=2)
        self.Sps = [ps("S0", [128, 1024]), ps("S1", [128, 1024])]
        self.bank = {i: ps(f"B{i}", [128, 512]) for i in (4, 5, 6, 7)}
        R = lambda n: Res(n)
        self.R_cst = R("cst")
        self.R_lam = R("lam")
        self.R_hT = [R(f"hT{b}") for b in range(8)]
        self.R_oT = [[R(f"oT{c}_{b}") for b in range(8)] for c in range(8)]
        self.R_KT = [R(f"KT{b}") for b in range(8)]
        self.R_V = [R(f"V{b}") for b in range(8)]
        self.R_QT = [R("QT0"), R("QT1")]
        self.R_PT = [R(f"PT{i}") for i in range(3)]
        self.R_wb = [R(f"wb{i}") for i in range(4)]
        self.R_wst = [R(f"wst{i}") for i in range(2)]
        self.R_scr = [R(f"scr{i}") for i in range(10)]
        self.R_sbA = [R("sbA0"), R("sbA1")]
        self.R_bias = [R("bias0"), R("bias1")]
        self.R_S = [R("S0"), R("S1")]
        self.R_bank = {i: R(f"B{i}") for i in (4, 5, 6, 7)}
        self.R_stat = [R(f"stat{i}") for i in range(8)]
        self.R_out = [R(f"out{t}") for t in range(32)]
        self.free_banks = [6, 7]
        self.free_scr = list(range(10))

    def cmat(self, k, n=128):
        return self.cst[0:n, k * 128:k * 128 + n]

    def gcol(self, k, n=128):
        return self.cst[0:n, GOFF + k:GOFF + k + 1]

    def scr(self):
        i = self.free_scr.pop(0)
        return self.SCR[:, i, :], self.R_scr[i]

    def rel(self, *rs):
        for r in rs:
            i = self.R_scr.index(r)
            assert i not in self.free_scr
            self.free_scr.append(i)

    def set_misc(self, lst):
        self.free_banks = list(lst)

    def mbank(self):
        i = self.free_banks.pop(0)
        return self.bank[i], self.R_bank[i]

    def relb(self, *rs):
        for r in rs:
            i = [k for k, v in self.R_bank.items() if v is r][0]
            assert i not in self.free_banks
            self.free_banks.append(i)

    def mm(self, out, lhsT, rhs, start, stop, reads, writes, tp=None):
        if tp is None:
            fn = lambda e: e.matmul(out, lhsT=lhsT, rhs=rhs, start=start, stop=stop)
        else:
            fn = lambda e: e.matmul(out, lhsT=lhsT, rhs=rhs, start=start, stop=stop, tile_position=tp)
        return self.S.op("pe", fn, reads=reads, writes=writes)

    def act(self, out, in_, func, reads, writes, scale=1.0, bias=None):
        if bias is None:
            fn = lambda e: e.activation(out=out, in_=in_, func=func, scale=scale)
        else:
            fn = lambda e: e.activation(out=out, in_=in_, func=func, scale=scale, bias=bias)
        return self.S.op("act", fn, reads=reads, writes=writes)

    def tt(self, q, out, in0, in1, op, reads, writes):
        return self.S.op(q, lambda e: e.tensor_tensor(out=out, in0=in0, in1=in1, op=op), reads=reads, writes=writes)

    def stt(self, q, out, in0, scalar, in1, op0, op1, reads, writes):
        return self.S.op(q, lambda e: e.scalar_tensor_tensor(out=out, in0=in0, scalar=scalar, in1=in1, op0=op0, op1=op1),
                         reads=reads, writes=writes)

    def ts(self, q, out, in0, s1, op0, reads, writes, s2=None, op1=None):
        if op1 is None:
            fn = lambda e: e.tensor_scalar(out=out, in0=in0, scalar1=s1, scalar2=None, op0=op0)
        else:
            fn = lambda e: e.tensor_scalar(out=out, in0=in0, scalar1=s1, scalar2=s2, op0=op0, op1=op1)
        return self.S.op(q, fn, reads=reads, writes=writes)

    def recip(self, out, in_, reads, writes):
        return self.S.op("dve", lambda e: e.reciprocal(out=out, in_=in_), reads=reads, writes=writes)

    def copy(self, q, out, in_, reads, writes):
        if q == "act":
            return self.act(out, in_, AF.Copy, reads, writes)
        return self.S.op(q, lambda e: e.tensor_copy(out=out, in_=in_), reads=reads, writes=writes)

    def dma(self, q, out, in_, reads, writes):
        return self.S.op(q, lambda e: e.dma_start(out=out, in_=in_), reads=reads, writes=writes, dma=True)

    def setup(self):
        self.dma("sp", self.cst[:], self.cst_d, [], [self.R_cst])
        self.dma("sp", self.LV[:], self.lamv_d.partition_broadcast(128), [], [self.R_lam])
        self.copy("dve", self.identb[:], self.cmat(C_ID), [self.R_cst], [self.R_cst])
        self.copy("dve", self.onesb[:], self.cmat(C_ONES), [self.R_cst], [self.R_cst])
        if 1 in self.layers:
            lc, LV = self.lamc, self.LV
            tmp, rt = self.scr()
            for j in range(2):
                self.tt("dve", tmp[:, 0:64], LV[:, 128 * j:128 * j + 64], LV[:, 128 * j + 64:128 * j + 128], ALU.mult,
                        [self.R_lam], [rt])
                self.S.op("dve", lambda e, j=j: e.reduce_sum(out=lc[:, j:j + 1], in_=tmp[:, 0:64], axis=AX.X),
                          reads=[rt], writes=[self.R_lam])
                self.act(lc[:, 2 + j:3 + j], lc[:, j:j + 1], AF.Exp, [self.R_lam], [self.R_lam])
            self.tt("dve", lc[:, 4:5], lc[:, 3:4], lc[:, 2:3], ALU.subtract, [self.R_lam], [self.R_lam])
            self.ts("dve", lc[:, 4:5], lc[:, 4:5], -LAM_INIT, ALU.add, [self.R_lam], [self.R_lam])
            self.ts("dve", lc[:, 5:6], self.gcol(G_SUB), 1.0 - LAM_INIT, ALU.mult, [self.R_lam, self.R_cst], [self.R_lam])
            self.rel(rt)

    def load_w(self, w, specs, M, nk=8, fold=None, dst=None):
        si = self.nxt("wst", 2)
        stage = self.wst[:, si, :].rearrange("p (c m) -> p c m", c=8)
        if dst is None:
            wi = self.nxt("wb", 4)
            wbv = self.wb[:, wi, :].rearrange("p (c m) -> p c m", c=8)
            rw = self.R_wb[wi]
        else:
            wbv, rw = dst
        wv = w.rearrange("(c p) n -> p c n", p=128)
        for (off, src, n) in specs:
            self.dma("sp", stage[:, 0:nk, off:off + n], wv[:, :, src:src + n], [], [self.R_wst[si]])
        for c in range(nk):
            if fold is not None:
                self.ts("dve", wbv[:, c, 0:M], stage[:, c, 0:M], self.gcol(fold + c), ALU.mult,
                        [self.R_wst[si], self.R_cst], [rw])
            else:
                self.copy("dve", wbv[:, c, 0:M], stage[:, c, 0:M], [self.R_wst[si]], [rw])
        return wbv, rw

    def xnorm_tile(self, t, xt, r_xt):
        i = t % 2
        sq = self.SCR[:, 6 + 2 * i:8 + 2 * i, :].rearrange("p a b -> p (a b)")
        r_sq = [self.R_scr[6 + 2 * i], self.R_scr[7 + 2 * i]]
        self.act(sq, xt, AF.Square, r_xt, r_sq)
        st = self.stat[:, i:i + 1]
        rs = self.R_stat[i]
        self.S.op("dve", lambda e: e.reduce_sum(out=st, in_=sq, axis=AX.X), reads=r_sq, writes=[rs])
        self.act(st, st, AF.Ln, [rs], [rs], scale=1.0 / DM, bias=EPS)
        self.act(st, st, AF.Exp, [rs], [rs], scale=-0.5)
        pi = self.nxt("PT", 3)
        xs = self.PT[:, pi, :]
        self.ts("dve", xs, xt, st, ALU.mult, r_xt + [rs], [self.R_PT[pi]])
        bkb = self.Sps[i][:, 0:512].bitcast(BF16)
        rb = self.R_S[i]
        for c in range(8):
            self.S.op("pe", lambda e, c=c: e.transpose(bkb[:, c * 128:(c + 1) * 128], xs[:, c * 128:(c + 1) * 128], self.identb[:]),
                      reads=[self.R_PT[pi], self.R_cst], writes=[rb])
        self.copy("dve", self.hT[:, :, t * 128:(t + 1) * 128], bkb.rearrange("p (c k) -> p c k", c=8),
                  [rb], [self.R_hT[t // 4]])

    def xt_view(self, t, nbuf=3):
        i = t % nbuf
        return (self.SCR[:, 2 * i:2 * i + 2, :].rearrange("p a b -> p (a b)"), [self.R_scr[2 * i], self.R_scr[2 * i + 1]])

    def prologue_from_x(self):
        for t in range(32):
            xt, r_xt = self.xt_view(t)
            self.dma("sp", xt, self.x[t * 128:(t + 1) * 128, :], [], r_xt)
            self.xnorm_tile(t, xt, r_xt)

    def out_proj(self, w_out, src, last, next_norm):
        rPB = self.R_KT + self.R_V
        for c in range(8):
            si = self.nxt("wst", 2)
            stage = self.wst[:, si, :]
            self.dma("sp", stage, w_out[c * 128:(c + 1) * 128, :], [], [self.R_wst[si]])
            self.copy("pool", self.WoB[:, c, :], stage, [self.R_wst[si]], rPB)
        stores = []
        nbuf = 3 if next_norm else 4
        for t in range(32):
            xt, r_xt = self.xt_view(t, nbuf)
            if src is self.out:
                self.dma("sp", xt, src[t * 128:(t + 1) * 128, :], [self.R_out[t]], r_xt)
            else:
                self.dma("sp", xt, src[t * 128:(t + 1) * 128, :], [], r_xt)
            pair = (4, 5) if t % 2 == 0 else (6, 7)
            for n in range(2):
                bk, rb = self.bank[pair[n]], self.R_bank[pair[n]]
                for c in range(8):
                    self.mm(bk[:], self.oT[:, c, t * 128:(t + 1) * 128], self.WoB[:, c, n * 512:(n + 1) * 512],
                            c == 0, c == 7, [self.R_oT[c][t // 4]] + rPB, [rb])
                self.tt("dve", xt[:, n * 512:(n + 1) * 512], bk[:], xt[:, n * 512:(n + 1) * 512], ALU.add,
                        [rb] + r_xt, r_xt)
            stores.append(self.dma("pool", self.out[t * 128:(t + 1) * 128, :], xt, r_xt, [self.R_out[t]]))
            if next_norm:
                self.xnorm_tile(t, xt, r_xt)
        return stores

    def fm_proj(self, bk, rb, wbv, rw, M, blk, nk=8, rows0=0, rhs=None, rrhs=None, first=True, last=True):
        tp = (0, rows0) if rows0 else None
        for c in range(nk):
            r = self.hT[:, c, blk * 512:(blk + 1) * 512] if rhs is None else rhs(c)
            rr = [self.R_hT[blk]] if rhs is None else rrhs
            self.mm(bk[rows0:rows0 + M, :], wbv[:, c, 0:M], r, first and c == 0, last and c == nk - 1, [rw] + rr, [rb], tp)

    def norm_chain_g(self, proj, n, ones_k, inv_d, gk, rot, blk, out_ap, out_res, lowbank=False):
        bk, rb = self.mbank()
        proj(bk, rb)
        yield
        sq, rsq = self.scr()
        self.act(sq[0:n], bk[0:n, :], AF.Square, [rb], [rsq])
        if lowbank:
            pc, rpc = self.scr()
            self.act(pc[0:n], bk[0:n, :], AF.Copy, [rb], [rpc])
            self.relb(rb)
            src, rsrc = pc[0:n], rpc
        else:
            src, rsrc = bk[0:n, :], rb
        yield
        b2, rb2 = self.mbank()
        self.mm(b2[0:n, :], self.cmat(ones_k, n), sq[0:n], True, True, [rsq, self.R_cst], [rb2])
        if rot is not None:
            rk, ti = rot
            tcs, rtc = self.scr()
            tss, rts = self.scr()
            self.dma("sp", tcs[0:n], self.tabs_d[ti, 0:n, blk * 512:(blk + 1) * 512], [], [rtc])
            self.dma("sp", tss[0:n], self.tabs_d[ti + 1, 0:n, blk * 512:(blk + 1) * 512], [], [rts])
        yield
        self.act(sq[0:n], b2[0:n, :], AF.Ln, [rb2], [rsq], scale=inv_d, bias=EPS)
        self.relb(rb2)
        self.act(sq[0:n], sq[0:n], AF.Exp, [rsq], [rsq], scale=-0.5)
        if rot is None:
            self.stt("dve", out_ap, src, self.gcol(gk, n), sq[0:n], ALU.mult, ALU.mult,
                     [rsrc, rsq, self.R_cst], out_res)
            self.rel(rsq)
            if lowbank:
                self.rel(rpc)
            else:
                self.relb(rb)
            return
        kn, rkn = self.scr()
        self.stt("dve", kn[0:n], src, self.gcol(gk, n), sq[0:n], ALU.mult, ALU.mult,
                 [rsrc, rsq, self.R_cst], [rkn])
        self.rel(rsq)
        if lowbank:
            self.rel(rpc)
        else:
            self.relb(rb)
        yield
        b3, rb3 = self.mbank()
        self.mm(b3[0:n, :], self.cmat(rk, n), kn[0:n], True, True, [rkn, self.R_cst], [rb3])
        yield
        self.tt("pool", kn[0:n], kn[0:n], tcs[0:n], ALU.mult, [rkn, rtc], [rkn])
        self.tt("dve", tss[0:n], b3[0:n, :], tss[0:n], ALU.mult, [rb3, rts], [rts])
        self.relb(rb3)
        self.tt("pool", out_ap, kn[0:n], tss[0:n], ALU.add, [rkn, rts], out_res)
        self.rel(rkn, rtc, rts)

    @staticmethod
    def drain(gen):
        for _ in gen:
            pass

    @staticmethod
    def staggered(gens, every=2):
        active = []
        for g in gens:
            active.append(g)
            for _ in range(every):
                for a in list(active):
                    try:
                        next(a)
                    except StopIteration:
                        active.remove(a)
        while active:
            for a in list(active):
                try:
                    next(a)
                except StopIteration:
                    active.remove(a)

    def gate_g(self, wg, rwg, M, p0, blk, state, act_recip=False):
        bk, rb = self.mbank()
        self.fm_proj(bk, rb, wg, rwg, M, blk, rows0=p0)
        sl = slice(p0, p0 + M)
        yield
        e, re_ = self.scr()
        gs, rgs = self.scr()
        self.act(e[sl], bk[sl, :], AF.Exp, [rb], [re_], scale=-1.0)
        self.act(gs[sl], bk[sl, :], AF.Copy, [rb], [rgs])
        self.relb(rb)
        state[("g", blk)] = (gs, rgs)
        yield
        if act_recip:
            self.act(e[sl], e[sl], AF.Ln, [re_], [re_], scale=1.0, bias=1.0)
            self.act(e[sl], e[sl], AF.Exp, [re_], [re_], scale=-1.0)
        else:
            self.ts("dve", e[sl], e[sl], 1.0, ALU.add, [re_], [re_])
            self.recip(e[sl], e[sl], [re_], [re_])
        self.tt("pool", gs[sl], gs[sl], e[sl], ALU.mult, [rgs, re_], [rgs])
        self.rel(re_)

    def gate_stage(self, wg, rwg, M, p0, blk):
        st = {}
        self.drain(self.gate_g(wg, rwg, M, p0, blk, st))
        return st[("g", blk)]

    def epilogue_ab(self, p0, M, chunk, blk, gs, rgs, extra_w=()):
        sl = slice(p0, p0 + M)
        cO, rcO = self.scr()
        cS, rcS = self.scr()
        self.act(cS[sl], self.bank[5][sl, :], AF.Ln, [self.R_bank[5]], [rcS])
        self.act(cO[sl], self.bank[4][sl, :], AF.Copy, [self.R_bank[4]], [rcO])
        self.act(cS[sl], cS[sl], AF.Exp, [rcS], [rcS], scale=-1.0)
        self.tt("pool", cO[sl], cO[sl], cS[sl], ALU.mult, [rcO, rcS], [rcO])
        self.tt("pool", self.oT[sl, chunk, blk * 512:(blk + 1) * 512], cO[sl], gs[sl], ALU.mult,
                [rcO, rgs], [self.R_oT[chunk][blk]] + list(extra_w))
        self.rel(rcO, rcS, rgs)

    def epilogue_merged(self, parts, p0, M, krows, chunk, blk, gs, rgs, extra_w=()):
        sl = slice(p0, p0 + M)
        cO, rcO = self.scr()
        cS, rcS = self.scr()
        for (bi, (ol, oh), (s0, s1)) in parts:
            self.act(cS[s0:s1], self.bank[bi][s0:s1, :], AF.Copy, [self.R_bank[bi]], [rcS])
            self.copy("dve", cO[ol:oh], self.bank[bi][ol:oh, :], [self.R_bank[bi]], [rcO])
        bm, rbm = self.mbank()
        k0, k1 = krows
        self.mm(bm[:], self.cst[k0:k1, C_SWAP * 128:(C_SWAP + 1) * 128], cS[k0:k1], True, True, [rcS, self.R_cst], [rbm])
        self.recip(cS[sl], bm[sl, :], [rbm], [rcS])
        self.relb(rbm)
        self.tt("pool", cO[sl], cO[sl], cS[sl], ALU.mult, [rcO, rcS], [rcO])
        self.tt("pool", self.oT[sl, chunk, blk * 512:(blk + 1) * 512], cO[sl], gs[sl], ALU.mult,
                [rcO, rgs], [self.R_oT[chunk][blk]] + list(extra_w))
        self.rel(rcO, rcS, rgs)

    def prime1(self, streams):
        s = streams[0]
        lo, hi = s["kt"]
        for u in range(2):
            for c in range(2):
                kc = 2 * u + c
                self.mm(self.Sps[u][:, c * 512:(c + 1) * 512], self.KT[lo:hi, kc * 128:(kc + 1) * 128], s["q"],
                        True, True, [self.R_KT[kc // 4], s["rq"]], [self.R_S[u]])

    def prime2(self, streams):
        for u in range(2):
            for j, s in enumerate(streams):
                lo, hi = s["kt"]
                self.mm(self.Sps[u][:, j * 512:(j + 1) * 512], self.KT[lo:hi, u * 128:(u + 1) * 128], s["q"],
                        True, True, [self.R_KT[u // 4], s["rq"]], [self.R_S[u]])

    def attn_block(self, streams, scale, hooks, primed=False):
        units = [(kg, j) for kg in range(16) for j in range(len(streams))]
        nu = len(units)

        def qk(u):
            kg, j = units[u]
            s = streams[j]
            lo, hi = s["kt"]
            si = u % 2
            for c in range(2):
                kc = 2 * kg + c
                self.mm(self.Sps[si][:, c * 512:(c + 1) * 512], self.KT[lo:hi, kc * 128:(kc + 1) * 128], s["q"],
                        True, True, [self.R_KT[kc // 4], s["rq"]], [self.R_S[si]])

        def rest(u):
            kg, j = units[u]
            s = streams[j]
            si = u % 2
            pi = self.nxt("PT", 3)
            self.act(self.PT[:, pi, :], self.Sps[si][:], AF.Exp, [self.R_S[si]], [self.R_PT[pi]], scale=scale)
            for c in range(2):
                kc = 2 * kg + c
                for (out_ap, lhs_fn, res, tp, rv) in s["pv"]:
                    self.mm(out_ap, lhs_fn(kc), self.PT[:, pi, c * 512:(c + 1) * 512], kc == 0, kc == 31,
                            [self.R_PT[pi]] + rv(kc), [res], tp)

        assert len(streams) == 1
        if not primed:
            qk(0)
            qk(1)
        for u in range(nu):
            rest(u)
            if u + 2 < nu:
                qk(u + 2)
            if u in hooks:
                hooks[u]()

    def attn_block2(self, streams, scale, hooks, primed=False):
        nu = 32
        npv = len(streams[0]["pv"])

        def qk(u):
            si = u % 2
            for j, s in enumerate(streams):
                lo, hi = s["kt"]
                self.mm(self.Sps[si][:, j * 512:(j + 1) * 512], self.KT[lo:hi, u * 128:(u + 1) * 128], s["q"],
                        True, True, [self.R_KT[u // 4], s["rq"]], [self.R_S[si]])

        def rest(u):
            si = u % 2
            pi = self.nxt("PT", 3)
            self.act(self.PT[:, pi, :], self.Sps[si][:], AF.Exp, [self.R_S[si]], [self.R_PT[pi]], scale=scale)
            for k in range(npv):
                for j, s in enumerate(streams):
                    (out_ap, lhs_fn, res, tp, rv) = s["pv"][k]
                    self.mm(out_ap, lhs_fn(u), self.PT[:, pi, j * 512:(j + 1) * 512], u == 0, u == nu - 1,
                            [self.R_PT[pi]] + rv(u), [res], tp)

        if not primed:
            qk(0)
            qk(1)
        for u in range(nu):
            rest(u)
            if u + 2 < nu:
                qk(u + 2)
            if u in hooks:
                hooks[u]()

    def kchain(self, w, rw, n_rows_proj, ones_k, inv_d, gk, rot, blk, M=128):
        proj = lambda bk, rb: self.fm_proj(bk, rb, w, rw, M, blk)
        return self.norm_chain_g(proj, M, ones_k, inv_d, gk, rot, blk,
                                 self.KT[0:M, blk * 512:(blk + 1) * 512], [self.R_KT[blk]])

    def qchain(self, w, rw, ones_k, inv_d, gk, rot, blk, state, M=128, nk=8, rhs=None, rrhs=None, lowbank=False):
        qi = self.nxt("QT", 2)
        state[("q", blk)] = qi
        proj = lambda bk, rb: self.fm_proj(bk, rb, w, rw, M, blk, nk=nk, rhs=rhs, rrhs=rrhs)
        return self.norm_chain_g(proj, M, ones_k, inv_d, gk, rot, blk, self.QTb[0:M, qi, :], [self.R_QT[qi]], lowbank=lowbank)

    @staticmethod
    def step(gen):
        try:
            next(gen)
        except StopIteration:
            pass

    def layer0(self):
        W = self.w_in_e
        for p in range(4):
            wk, rwk = self.load_w(W, [(0, 512 + 128 * p, 128)], 128, fold=G_NE)
            wv, rwv = self.load_w(W, [(0, 1024 + 128 * p, 128)], 128, fold=G_NE)
            wq, rwq = self.load_w(W, [(0, 128 * p, 128)], 128, fold=G_NE)
            wg, rwg = self.load_w(W, [(0, 2304 + 128 * p, 128)], 128, fold=G_NE)
            self.load_biasA(p, 2, 0)
            self.set_misc([4, 5, 6, 7])
            state = {}
            self.staggered([self.kchain(wk, rwk, 128, C_BLK64, 1.0 / 64, G_KA, None, blk) for blk in range(8)]
                           + [self.qchain(wq, rwq, C_BLK64, 1.0 / 64, G_QA, None, 0, state),
                              self.gate_g(wg, rwg, 128, 0, 0, state, act_recip=True)])
            for blk in range(8):
                self.v_block(wv, rwv, blk, 128)
            self.set_misc([6, 7])
            cur = {"cls": None}
            info = {}
            gates = {}

            def qk(t):
                blk, tl = divmod(t, 4)
                qi = state[("q", blk)]
                kt0 = a_kt0(t)
                for slot in range(5):
                    kt = kt0 + slot
                    for j in range(2):
                        lo, hi = 64 * j, 64 * j + 64
                        self.mm(self.Sps[j][:, slot * 128:(slot + 1) * 128], self.KT[lo:hi, kt * 128:(kt + 1) * 128],
                                self.QTb[lo:hi, qi, tl * 128:(tl + 1) * 128], True, True,
                                [self.R_KT[kt // 4], self.R_QT[qi]], [self.R_S[j]])

            def soft(t):
                cls = a_class(t)
                if cls == 2:
                    bi = 0
                else:
                    bi = 1
                    if cur["cls"] != cls:
                        self.load_biasA(p, cls, 1)
                        cur["cls"] = cls
                pis = []
                for j in range(2):
                    ai = self.nxt("sbA", 2)
                    self.stt("dve", self.sbA[:, ai, :], self.Sps[j][:, 0:640], 0.125, self.biasT[:, bi, j, :],
                             ALU.mult, ALU.add, [self.R_S[j], self.R_bias[bi]], [self.R_sbA[ai]])
                    pi = self.nxt("PT", 3)
                    self.act(self.PT[:, pi, 0:640], self.sbA[:, ai, :], AF.Exp, [self.R_sbA[ai]], [self.R_PT[pi]])
                    pis.append(pi)
                info[t] = pis

            def pv(t):
                blk, tl = divmod(t, 4)
                kt0 = a_kt0(t)
                pis = info[t]
                for slot in range(5):
                    kt = kt0 + slot
                    for bank in (4, 5):
                        for j in range(2):
                            lo, hi = 64 * j, 64 * j + 64
                            tp = (0, lo) if lo else None
                            rhs = self.PT[:, pis[j], slot * 128:(slot + 1) * 128]
                            if bank == 4:
                                lhsT, rr = self.V[:, kt, lo:hi], [self.R_V[kt // 4]]
                            else:
                                lhsT, rr = self.onesb[:, 0:64], [self.R_cst]
                            self.mm(self.bank[bank][lo:hi, tl * 128:(tl + 1) * 128], lhsT, rhs, slot == 0, slot == 4,
                                    [self.R_PT[pis[j]]] + rr, [self.R_bank[bank]], tp)

            nxtq = self.qchain(wq, rwq, C_BLK64, 1.0 / 64, G_QA, None, 1, state)
            nxtg = self.gate_g(wg, rwg, 128, 0, 1, state, act_recip=True)
            qk(0)
            for t in range(32):
                blk, tl = divmod(t, 4)
                soft(t)
                if t + 1 < 32:
                    if tl == 3:
                        self.drain(nxtq)
                    qk(t + 1)
                pv(t)
                if tl < 3:
                    self.step(nxtq)
                    self.step(nxtg)
                else:
                    self.drain(nxtg)
                    self.epilogue_ab(0, 128, p, blk, *state[("g", blk)])
                    nxtq = self.qchain(wq, rwq, C_BLK64, 1.0 / 64, G_QA, None, blk + 2, state) if blk + 2 < 8 else iter(())
                    nxtg = self.gate_g(wg, rwg, 128, 0, blk + 2, state, act_recip=True) if blk + 2 < 8 else iter(())
        self.va_ones()
        for g in range(2):
            wv, rwv = self.load_w(W, [(0, 2176 + 64 * g, 64)], 64, fold=G_NE)
            for blk in range(8):
                self.v_block(wv, rwv, blk, 64, dst=self.VA[:, :, 64:128])
            wk, rwk = self.load_w(W, [(0, 2048 + 64 * g, 64), (64, 2048 + 64 * g, 64)], 128, fold=G_NE)
            self.set_misc([4, 5, 6, 7])
            self.staggered([self.kchain(wk, rwk, 128, C_BLK64, 1.0 / 64, G_KB, (C_RB, 0), blk) for blk in range(8)])
            self.set_misc([6, 7])
            for sp_ in range(2):
                cq = 2 * g + sp_
                chunk = 4 + cq
                wq, rwq = self.load_w(W, [(0, 1536 + 128 * cq, 128)], 128, fold=G_NE)
                wg, rwg = self.load_w(W, [(0, 2304 + 128 * chunk, 128)], 128, fold=G_NE)
                state = {}
                extra = self.R_bias if chunk >= 6 else ()
                self.drain(self.qchain(wq, rwq, C_BLK64, 1.0 / 64, G_QB, (C_RB, 0), 0, state))
                def mk(blk):
                    qi = state[("q", blk)]
                    streams = []
                    for j in range(2):
                        lo, hi = 64 * j, 64 * j + 64
                        c0 = 64 if j == 0 else 0
                        streams.append(dict(
                            kt=(lo, hi), q=self.QTb[lo:hi, qi, :], rq=self.R_QT[qi],
                            pv=[(self.bank[4 + j][:], (lambda kc, c0=c0: self.VA[:, kc, c0:c0 + 128]), self.R_bank[4 + j], None,
                                 (lambda kc: [self.R_V[kc // 4]]))]))
                    return streams

                self.prime2(mk(0))
                for blk in range(8):
                    streams = mk(blk)
                    nxtq = self.qchain(wq, rwq, C_BLK64, 1.0 / 64, G_QB, (C_RB, 0), blk + 1, state) if blk + 1 < 8 else iter(())

                    gg = self.gate_g(wg, rwg, 128, 0, blk, state)
                    hooks = {}
                    for u in (2, 4, 6):
                        hooks[u] = (lambda gg=gg: self.step(gg))
                    for u in (9, 11, 13, 15, 17, 19):
                        hooks[u] = (lambda nq=nxtq: self.step(nq))
                    self.attn_block2(streams, 0.125, hooks, primed=True)
                    self.drain(gg)
                    self.drain(nxtq)
                    if blk + 1 < 8:
                        self.prime2(mk(blk + 1))
                    gs, rgs = state[("g", blk)]
                    self.epilogue_merged([(4, (0, 64), (64, 128)), (5, (64, 128), (0, 64))], 0, 128, (0, 128),
                                         chunk, blk, gs, rgs, extra)

    def load_biasA(self, p, cls, bi):
        for j in range(2):
            self.dma("sp", self.biasT[:, bi, j, :], self.biasA_d[2 * p + j, cls], [], [self.R_bias[bi]])

    def v_block(self, wv, rwv, blk, ncol, rhs_nk=8, lhs=None, rl=None, dst=None):
        bk, rb = self.mbank()
        for tl in range(4):
            t = blk * 4 + tl
            for c in range(rhs_nk):
                l = self.hT[:, c, t * 128:(t + 1) * 128] if lhs is None else lhs(t)
                rr = [self.R_hT[blk]] if lhs is None else rl(blk)
                self.mm(bk[:, tl * ncol:(tl + 1) * ncol], l, wv[:, c, 0:ncol], c == 0, c == rhs_nk - 1, [rwv] + rr, [rb])
        dv = self.V[:, blk * 4:(blk + 1) * 4, 0:ncol] if dst is None else dst[:, blk * 4:(blk + 1) * 4, :]
        self.copy("act", dv, bk[:, 0:4 * ncol].rearrange("p (t c) -> p t c", t=4), [rb], [self.R_V[blk]])
        self.relb(rb)

    def va_ones(self):
        for sl in (slice(0, 64), slice(128, 192)):
            self.S.op("pool", lambda e, sl=sl: e.memset(self.VA[:, :, sl], 1.0), reads=[], writes=self.R_V)

    def layer1(self):
        W = self.w_in_o
        CQN = [self.oT[:, 5, :], self.oT[:, 6, :]]
        CKVN = self.oT[:, 7, :]
        r_cqn = lambda blk: [self.R_oT[5][blk], self.R_oT[6][blk]]
        r_ckvn = lambda blk: [self.R_oT[7][blk]]
        wc0, rwc0 = self.load_w(W, [(0, 0, 128)], 128, fold=G_NO)
        wc1, rwc1 = self.load_w(W, [(0, 128, 128)], 128, fold=G_NO)
        wkv, rwkv = self.load_w(W, [(0, 256, 128)], 128, fold=G_NO)
        for blk in range(8):
            b6, r6 = self.bank[6], self.R_bank[6]
            b7, r7 = self.bank[7], self.R_bank[7]
            b5, r5 = self.bank[5], self.R_bank[5]
            self.fm_proj(b6, r6, wc0, rwc0, 128, blk)
            self.fm_proj(b7, r7, wc1, rwc1, 128, blk)
            s0, rs0 = self.scr()
            s1, rs1 = self.scr()
            self.act(s0, b6[:], AF.Square, [r6], [rs0])
            self.act(s1, b7[:], AF.Square, [r7], [rs1])
            self.mm(b5[:], self.cmat(C_ONES), s0, True, False, [rs0, self.R_cst], [r5])
            self.mm(b5[:], self.cmat(C_ONES), s1, False, True, [rs1, self.R_cst], [r5])
            self.act(s0, b5[:], AF.Ln, [r5], [rs0], scale=1.0 / 256, bias=EPS)
            self.act(s0, s0, AF.Exp, [rs0], [rs0], scale=-0.5)
            sl = slice(blk * 512, (blk + 1) * 512)
            self.stt("dve", CQN[0][:, sl], b6[:], self.gcol(G_CQ0), s0, ALU.mult, ALU.mult, [r6, rs0, self.R_cst], [self.R_oT[5][blk]])
            self.stt("dve", CQN[1][:, sl], b7[:], self.gcol(G_CQ1), s0, ALU.mult, ALU.mult, [r7, rs0, self.R_cst], [self.R_oT[6][blk]])
            self.rel(rs0, rs1)
        self.set_misc([4, 5, 6, 7])
        self.staggered([self.norm_chain_g((lambda bk, rb, blk=blk: self.fm_proj(bk, rb, wkv, rwkv, 128, blk)),
                                          128, C_ONES, 1.0 / 128, G_CKV, None, blk,
                                          CKVN[:, blk * 512:(blk + 1) * 512], [self.R_oT[7][blk]]) for blk in range(8)])
        self.va_ones()
        wpe, rwpe = self.load_w(W, [(0, 384, 32)], 32, fold=G_NO, dst=(self.wpe_t, self.R_wpe))
        for h in range(8):
            half = 64 * (h % 2)
            chunk = h // 2
            wkn, rwkn = self.load_w(self.w_ckv_b, [(0, 128 * h, 128)], 128, nk=1)
            wqb, rwqb = self.load_w(self.w_cq_b, [(0, 96 * h, 96)], 96, nk=2)
            wg, rwg = self.load_w(W, [(0, 1952 + 64 * h, 64)], 64, fold=G_NO)
            self.set_misc([4, 5, 6, 7])

            def kproj(bk, rb, blk, wkn=wkn, rwkn=rwkn):
                sl = slice(blk * 512, (blk + 1) * 512)
                self.mm(bk[0:64, :], wkn[:, 0, 0:64], CKVN[:, sl], True, True, [rwkn] + r_ckvn(blk), [rb])
                self.fm_proj(bk, rb, wpe, rwpe, 32, blk, rows0=64)

            state = {}

            def qch(blk, wqb=wqb, rwqb=rwqb):
                return self.qchain(wqb, rwqb, C_ONES96, 1.0 / 96, G_QC, (C_RC, 2), blk, state, M=96, nk=2,
                                   rhs=(lambda c: CQN[c][:, blk * 512:(blk + 1) * 512]), rrhs=r_cqn(blk))

            self.staggered([self.norm_chain_g((lambda bk, rb, blk=blk: kproj(bk, rb, blk)), 96, C_ONES96, 1.0 / 96, G_KC,
                                              (C_RC, 2), blk, self.KT[0:96, blk * 512:(blk + 1) * 512], [self.R_KT[blk]])
                            for blk in range(8)] + [qch(0)])
            for blk in range(8):
                self.v_block(wkn[:, :, 64:128], rwkn, blk, 64, rhs_nk=1,
                             lhs=(lambda t: CKVN[:, t * 128:(t + 1) * 128]), rl=r_ckvn, dst=self.VA[:, :, 64:128])
            self.set_misc([5, 6, 7])
            c0 = 64 if half == 0 else 0
            other = (64, 128) if half == 0 else (0, 64)
            def mk(blk, c0=c0):
                qi = state[("q", blk)]
                return [dict(
                    kt=(0, 96), q=self.QTb[0:96, qi, :], rq=self.R_QT[qi],
                    pv=[(self.bank[4][:], (lambda kc, c0=c0: self.VA[:, kc, c0:c0 + 128]), self.R_bank[4], None,
                         (lambda kc: [self.R_V[kc // 4]]))])]

            self.prime1(mk(0))
            for blk in range(8):
                streams = mk(blk)
                nxtq = qch(blk + 1) if blk + 1 < 8 else iter(())

                gg = self.gate_g(wg, rwg, 64, half, blk, state)
                hooks = {}
                for u in (1, 2, 3):
                    hooks[u] = (lambda gg=gg: self.step(gg))
                for u in (5, 6, 7, 8, 9, 10):
                    hooks[u] = (lambda nq=nxtq: self.step(nq))
                self.attn_block(streams, 96 ** -0.5, hooks, primed=True)
                self.drain(gg)
                self.drain(nxtq)
                if blk + 1 < 8:
                    self.prime1(mk(blk + 1))
                gs, rgs = state[("g", blk)]
                self.epilogue_merged([(4, (half, half + 64), other)], half, 64, other, chunk, blk, gs, rgs)
        for h in range(4):
            chunk = 4 + h
            wk, rwk = self.load_w(W, [(0, 928 + 128 * h, 128)], 128, fold=G_NO)
            wv, rwv = self.load_w(W, [(0, 1440 + 128 * h, 128)], 128, fold=G_NO)
            wq, rwq = self.load_w(W, [(0, 416 + 128 * h, 128)], 128, fold=G_NO)
            wg, rwg = self.load_w(W, [(0, 1952 + 512 + 128 * h, 128)], 128, fold=G_NO)
            self.set_misc([4, 5, 6, 7])
            state = {}
            self.staggered([self.kchain(wk, rwk, 128, C_BLK64, 1.0 / 64, G_KD, (C_RD, 4), blk) for blk in range(8)]
                           + [self.qchain(wq, rwq, C_BLK64, 1.0 / 64, G_QD, (C_RD, 4), 0, state),
                              self.gate_g(wg, rwg, 128, 0, 0, state)])
            for blk in range(8):
                self.v_block(wv, rwv, blk, 128)
            self.set_misc([7])
            prev_epi = None
            def mk(blk):
                qi = state[("q", blk)]
                streams = []
                for j in range(2):
                    lo, hi = 64 * j, 64 * j + 64
                    tp = (0, lo) if lo else None
                    streams.append(dict(
                        kt=(lo, hi), q=self.QTb[lo:hi, qi, :], rq=self.R_QT[qi],
                        pv=[(self.bank[4 + j][:], (lambda kc: self.V[:, kc, :]), self.R_bank[4 + j], None,
                             (lambda kc: [self.R_V[kc // 4]])),
                            (self.bank[6][lo:hi, :], (lambda kc: self.onesb[:, 0:64]), self.R_bank[6], tp,
                             (lambda kc: [self.R_cst]))]))
                return streams

            self.prime2(mk(0))
            for blk in range(8):
                streams = mk(blk)
                nxtq = (self.qchain(wq, rwq, C_BLK64, 1.0 / 64, G_QD, (C_RD, 4), blk + 1, state, lowbank=True)
                        if blk + 1 < 8 else iter(()))
                nxtg = self.gate_g(wg, rwg, 128, 0, blk + 1, state) if blk + 1 < 8 else iter(())
                hooks = {}
                if prev_epi is not None:
                    for u in (6, 7, 8):
                        hooks[u] = (lambda ep=prev_epi: self.step(ep))
                for u in (9, 11, 13):
                    hooks[u] = (lambda gg=nxtg: self.step(gg))
                for u in (15, 17, 19, 21, 23, 25):
                    hooks[u] = (lambda nq=nxtq: self.step(nq))
                self.attn_block2(streams, 0.125, hooks, primed=True)
                if prev_epi is not None:
                    self.drain(prev_epi)
                self.drain(nxtg)
                self.drain(nxtq)
                if blk + 1 < 8:
                    self.prime2(mk(blk + 1))
                prev_epi = self.d_epilogue_g(chunk, blk, *state[("g", blk)])
                self.step(prev_epi)
            self.drain(prev_epi)

    def d_epilogue_g(self, chunk, blk, gs, rgs):
        c1, rc1 = self.scr()
        c2, rc2 = self.scr()
        c3, rc3 = self.scr()
        self.act(c3, self.bank[6][:], AF.Copy, [self.R_bank[6]], [rc3])
        self.act(c1, self.bank[4][:], AF.Copy, [self.R_bank[4]], [rc1])
        self.act(c2, self.bank[5][:], AF.Copy, [self.R_bank[5]], [rc2])
        self.recip(c3, c3, [rc3], [rc3])
        bsw, rsw = self.mbank()
        self.mm(bsw[:], self.cmat(C_SWAP), c3, True, True, [rc3, self.R_cst], [rsw])
        self.tt("pool", c1[0:64], c1[0:64], c3[0:64], ALU.mult, [rc1, rc3], [rc1])
        self.tt("dve", c1[64:128], c1[64:128], bsw[64:128, :], ALU.mult, [rc1, rsw], [rc1])
        self.tt("dve", c2[0:64], c2[0:64], bsw[0:64, :], ALU.mult, [rc2, rsw], [rc2])
        self.relb(rsw)
        self.tt("pool", c2[64:128], c2[64:128], c3[64:128], ALU.mult, [rc2, rc3], [rc2])
        self.stt("dve", c1, c2, self.lamc[:, 4:5], c1, ALU.mult, ALU.add, [rc1, rc2, self.R_lam], [rc1])
        self.rel(rc3)
        yield
        self.act(c2, c1, AF.Square, [rc1], [rc2])
        yield
        bq, rq = self.mbank()
        self.mm(bq[:], self.cmat(C_ONES), c2, True, True, [rc2, self.R_cst], [rq])
        yield
        self.act(c2, bq[:], AF.Ln, [rq], [rc2], scale=1.0 / 128, bias=EPS)
        self.relb(rq)
        self.act(c2, c2, AF.Exp, [rc2], [rc2], scale=-0.5)
        self.stt("dve", c1, c1, self.lamc[:, 5:6], c2, ALU.mult, ALU.mult, [rc1, rc2, self.R_lam], [rc1])
        self.tt("pool", self.oT[:, chunk, blk * 512:(blk + 1) * 512], c1, gs, ALU.mult,
                [rc1, rgs], [self.R_oT[chunk][blk]])
        self.rel(rc1, rc2, rgs)

    def build(self):
        with ExitStack() as st:
            self.alloc(st)
            self.setup()
            self.prologue_from_x()
            stores = []
            if 0 in self.layers:
                self.layer0()
                stores = self.out_proj(self.w_out_e, self.x, last=(1 not in self.layers), next_norm=(1 in self.layers))
                src = self.out
            else:
                src = self.x
            if 1 in self.layers:
                self.layer1()
                stores = self.out_proj(self.w_out_o, src, last=True, next_norm=False)
            self.S.run(self.nc, st, final_waits=stores)
        return self.nc


_CACHE = {}


def _program(layers):
    if layers not in _CACHE:
        _CACHE[layers] = Bld(layers).build()
    return _CACHE[layers]


def kernel(**inputs):
    inp = {k: np.ascontiguousarray(np.asarray(v, dtype=np.float32)) for k, v in inputs.items()}
    cst, tabs, biasA, lamv = host_constants(inp)
    shared = {
        "w_in_e": inp["w_in_e"][0], "w_out_e": inp["w_out_e"][0], "w_in_o": inp["w_in_o"][0],
        "w_cq_b": inp["w_cq_b"][0], "w_ckv_b": inp["w_ckv_b"][0], "w_out_o": inp["w_out_o"][0],
        "cst": cst, "tabs": tabs, "biasA": biasA, "lamv": lamv,
    }
    x = inp["x"]
    nb = x.shape[0]
    nc = _program((0, 1))
    in_maps = [dict(shared, x=np.ascontiguousarray(x[b])) for b in range(nb)]
    res = run_bass_kernel_spmd(nc, in_maps, core_ids=list(range(nb)))
    return np.stack([np.asarray(r["out"], dtype=np.float32) for r in res.results], axis=0)
```
